# Optimizing a Trainium2 kernel written in Bass

```python
import math
import jax, jax.numpy as jnp
from jax import lax
import numpy as np

D_MODEL = 1024
BATCH = 8
SEQ = 2048
DEPTH = 2

MIX_WIDTH = D_MODEL
A_HEADS = 4
A_HEAD_DIM = MIX_WIDTH // 2 // A_HEADS
A_WIDTH = A_HEADS * A_HEAD_DIM
B_HEADS = 4
B_HEAD_DIM = MIX_WIDTH // 2 // B_HEADS
B_WIDTH = B_HEADS * B_HEAD_DIM
IN_AB = A_WIDTH + 2 * B_WIDTH
CONV_WIDTH = 31
POOL_WINDOWS = (2, 4, 8, 16)
C_GROUPS = len(POOL_WINDOWS)
C_GROUP_DIM = D_MODEL // C_GROUPS
D_FF = int(math.ceil((8 * D_MODEL / 3) / 256) * 256)
N_EVEN = (DEPTH + 1) // 2
N_ODD = DEPTH // 2
RMS_EPS = 1e-6
LN_EPS = 1e-5

kernel_name = "hybrid_fnet_conformer_poolformer_encoder"


def rmsnorm(x, g):
    xf = x.astype(jnp.float32)
    inv = lax.rsqrt(jnp.mean(xf * xf, axis=-1, keepdims=True) + RMS_EPS)
    return (xf * inv).astype(x.dtype) * g


def layernorm(x, g, b):
    xf = x.astype(jnp.float32)
    mu = jnp.mean(xf, axis=-1, keepdims=True)
    var = jnp.mean(jnp.square(xf - mu), axis=-1, keepdims=True)
    return ((xf - mu) * lax.rsqrt(var + LN_EPS)).astype(x.dtype) * g + b


def swiglu_ffn(h, w_gate, w_up, w_down):
    return (jax.nn.silu(h @ w_gate) * (h @ w_up)) @ w_down


def fnet_heads(a, fnet_map):
    bsz, seq, _ = a.shape
    a4 = a.reshape(bsz, seq, A_HEADS, A_HEAD_DIM).astype(jnp.float32)
    f = jnp.fft.fft2(a4, axes=(1, 3), norm="ortho").real.astype(a.dtype)
    y = jnp.einsum("bshd,hde->bshe", f, fnet_map)
    return y.reshape(bsz, seq, A_WIDTH)


def conformer_conv_heads(v, gate, conv_w, conv_b, ln_g, ln_b):
    u = v * jax.nn.sigmoid(gate)
    pad = CONV_WIDTH // 2
    u = lax.conv_general_dilated(
        u, conv_w[:, None, :].astype(u.dtype),
        window_strides=(1,), padding=[(pad, pad)],
        dimension_numbers=("NWC", "WIO", "NWC"),
        feature_group_count=B_WIDTH) + conv_b
    bsz, seq, _ = u.shape
    u = u.reshape(bsz, seq, B_HEADS, B_HEAD_DIM)
    u = layernorm(u, ln_g.reshape(B_HEADS, B_HEAD_DIM), ln_b.reshape(B_HEADS, B_HEAD_DIM))
    return jax.nn.silu(u).reshape(bsz, seq, B_WIDTH)


def centred_pool_minus_self(u, window):
    seq = u.shape[1]
    uf = u.astype(jnp.float32)
    cs = jnp.concatenate([jnp.zeros_like(uf[:, :1]), lax.cumsum(uf, axis=1)], axis=1)
    pos = jnp.arange(seq, dtype=jnp.int32)
    lo = jnp.clip(pos - window // 2, 0, seq)
    hi = jnp.clip(pos - window // 2 + window, 0, seq)
    win_sum = jnp.take(cs, hi, axis=1) - jnp.take(cs, lo, axis=1)
    cnt = (hi - lo).astype(jnp.float32)[None, :, None]
    return (win_sum / cnt - uf).astype(u.dtype)


def pool_mixer(h, pool_map, pool_scale):
    outs = []
    for gi, w in enumerate(POOL_WINDOWS):
        hg = h[..., gi * C_GROUP_DIM:(gi + 1) * C_GROUP_DIM]
        pg = centred_pool_minus_self(hg, w)
        outs.append(pg @ pool_map[gi])
    return jnp.concatenate(outs, axis=-1) * pool_scale


def setup_inputs(seed: int = 0) -> dict:
    key = jax.random.key(seed)
    ks = jax.random.split(key, 20)
    f32 = jnp.float32
    nrm = lambda k, shape, fan_in: jax.random.normal(k, shape, f32) * (fan_in ** -0.5)
    gain = lambda k, shape: 1.0 + 0.05 * jax.random.normal(k, shape, f32)
    return {
        "x": jax.random.normal(ks[0], (BATCH, SEQ, D_MODEL), f32),
        "norm_mix_g": gain(ks[1], (DEPTH, D_MODEL)),
        "norm_ffn_g": gain(ks[2], (DEPTH, D_MODEL)),
        "w_in_ab": nrm(ks[3], (N_EVEN, D_MODEL, IN_AB), D_MODEL),
        "fnet_map": nrm(ks[4], (N_EVEN, A_HEADS, A_HEAD_DIM, A_HEAD_DIM), A_HEAD_DIM),
        "conv_w": nrm(ks[5], (N_EVEN, CONV_WIDTH, B_WIDTH), CONV_WIDTH),
        "conv_b": 0.02 * jax.random.normal(ks[6], (N_EVEN, B_WIDTH), f32),
        "conv_ln_g": gain(ks[7], (N_EVEN, B_WIDTH)),
        "conv_ln_b": 0.02 * jax.random.normal(ks[8], (N_EVEN, B_WIDTH), f32),
        "w_out_ab": nrm(ks[9], (N_EVEN, MIX_WIDTH, D_MODEL), MIX_WIDTH),
        "pool_map": nrm(ks[10], (N_ODD, C_GROUPS, C_GROUP_DIM, C_GROUP_DIM), C_GROUP_DIM),
        "pool_scale": 1.0 + 0.1 * jax.random.normal(ks[11], (N_ODD, D_MODEL), f32),
        "ffn_w_gate": nrm(ks[12], (DEPTH, D_MODEL, D_FF), D_MODEL),
        "ffn_w_up": nrm(ks[13], (DEPTH, D_MODEL, D_FF), D_MODEL),
        "ffn_w_down": nrm(ks[14], (DEPTH, D_FF, D_MODEL), D_FF),
        "final_g": gain(ks[15], (D_MODEL,)),
    }


def reference(x, norm_mix_g, norm_ffn_g, w_in_ab, fnet_map, conv_w, conv_b, conv_ln_g,
              conv_ln_b, w_out_ab, pool_map, pool_scale, ffn_w_gate, ffn_w_up, ffn_w_down,
              final_g):
    for layer in range(DEPTH):
        h = rmsnorm(x, norm_mix_g[layer])
        if layer % 2 == 0:
            e = layer // 2
            p = h @ w_in_ab[e]
            ya = fnet_heads(p[..., :A_WIDTH], fnet_map[e])
            yb = conformer_conv_heads(p[..., A_WIDTH:A_WIDTH + B_WIDTH],
                                      p[..., A_WIDTH + B_WIDTH:],
                                      conv_w[e], conv_b[e], conv_ln_g[e], conv_ln_b[e])
            y = jnp.concatenate([ya, yb], axis=-1) @ w_out_ab[e]
        else:
            o = layer // 2
            y = pool_mixer(h, pool_map[o], pool_scale[o])
        x = x + y
        h = rmsnorm(x, norm_ffn_g[layer])
        x = x + swiglu_ffn(h, ffn_w_gate[layer], ffn_w_up[layer], ffn_w_down[layer])
    return rmsnorm(x, final_g)
```

```python
import contextlib
import numpy as np
import ml_dtypes
import concourse.bass as bass
import concourse.mybir as mybir
from concourse.bass_utils import run_bass_kernel_spmd

F32 = mybir.dt.float32
BF16 = mybir.dt.bfloat16
AF = mybir.ActivationFunctionType
ALU = mybir.AluOpType
AX = mybir.AxisListType

S = 2048
D = 1024
NCH = 8
TW = 512
NTT = 4
DFF = 2816
NF = 22
RMS_EPS = 1e-6
LN_EPS = 1e-5
ESZ = {F32: 4, BF16: 2}

C_ID = 0
C_CD = 128
C_SD = 256
C_GAIN = 384
C_CW = 416
C_CB = 540
C_LG = 544
C_LB = 548
C_PS = 552
C_IC = 560
NS = 624

GRAN = 256
SEM_CAP = 3000


class Buf:
    def __init__(self, name, handle, base_dt):
        self.name = name
        self.h = handle
        self.base_dt = base_dt
        self.besz = ESZ[base_dt]
        self.state = {}


class View:
    __slots__ = ("ap", "buf", "lo", "hi")

    def __init__(self, ap, buf, lo, hi):
        self.ap, self.buf, self.lo, self.hi = ap, buf, lo, hi


class T:
    def __init__(self, buf, off, dt):
        self.buf, self.off, self.dt, self.esz = buf, off, dt, ESZ[dt]

    def __call__(self, lo, n):
        b0 = self.off + lo * self.esz
        b1 = b0 + n * self.esz
        bes = self.buf.besz
        assert b0 % bes == 0 and b1 % bes == 0, (self.buf.name, b0, b1)
        ap = self.buf.h[:, b0 // bes:b1 // bes]
        if self.dt != self.buf.base_dt:
            assert b0 % 4 == 0 and b1 % 4 == 0
            ap = ap.bitcast(self.dt)
        return View(ap, self.buf, b0, b1)


class Op:
    __slots__ = ("eng", "fn", "deps", "dma_key", "dma_val", "idx", "sig")


class Prog:
    ENGS = ("pe", "act", "dve", "pool", "sp")

    def __init__(self):
        self.ops = {e: [] for e in self.ENGS}
        self.dma_cnt = {}
        self.nbank = 0

    def _tok(self, op):
        if op.dma_key is not None:
            return ("d", op.dma_key, op.dma_val)
        return ("e", op.eng, op.idx)

    def add(self, eng, fn, reads=(), writes=(), dma_key=None, extra_deps=()):
        op = Op()
        op.eng, op.fn, op.dma_key, op.sig = eng, fn, dma_key, False
        op.idx = len(self.ops[eng])
        op.dma_val = None
        if dma_key is not None:
            self.dma_cnt[dma_key] = self.dma_cnt.get(dma_key, 0) + 16
            op.dma_val = self.dma_cnt[dma_key]
        tok = self._tok(op)
        deps = set(extra_deps)
        for v in reads:
            st = v.buf.state
            for g in range(v.lo // GRAN, (v.hi - 1) // GRAN + 1):
                ent = st.get(g)
                if ent is None:
                    ent = st[g] = [None, []]
                if ent[0] is not None:
                    deps.add(ent[0])
                ent[1].append(tok)
        for v in writes:
            st = v.buf.state
            for g in range(v.lo // GRAN, (v.hi - 1) // GRAN + 1):
                ent = st.get(g)
                if ent is None:
                    ent = st[g] = [None, []]
                if ent[0] is not None:
                    deps.add(ent[0])
                for r in ent[1]:
                    deps.add(r)
                ent[0] = tok
                ent[1] = []
        deps.discard(tok)
        op.deps = [d for d in deps if not (d[0] == "e" and d[1] == eng)]
        self.ops[eng].append(op)
        return tok

    def bank(self):
        b = self.nbank % 8
        self.nbank += 1
        return b

    def emit(self, nc, stack):
        for e in self.ENGS:
            for op in self.ops[e]:
                for d in op.deps:
                    if d[0] == "e":
                        self.ops[d[1]][d[2]].sig = True
        signum = {}
        nsig = {}
        for e in self.ENGS:
            n = 0
            for op in self.ops[e]:
                if op.sig and op.dma_key is None:
                    n += 1
                    signum[(e, op.idx)] = n
            nsig[e] = n
        esems = {}
        for e in self.ENGS:
            k = (nsig[e] + SEM_CAP - 1) // SEM_CAP
            esems[e] = [stack.enter_context(nc.semaphore("p_%s_%d" % (e, i))) for i in range(max(k, 1))]
        dsems = {}
        for key in self.dma_cnt:
            dsems[key] = stack.enter_context(nc.semaphore("d_" + "_".join(str(x) for x in key)))
        block = stack.enter_context(nc.Block())

        def replay(ename, e):
            waited = {}
            for op in self.ops[ename]:
                need = {}
                for d in op.deps:
                    if d[0] == "e":
                        n = signum[(d[1], d[2])]
                        key = ("e", d[1])
                        val = n
                    else:
                        key = ("d", d[1])
                        val = d[2]
                    if waited.get(key, 0) >= val:
                        continue
                    if need.get(key, 0) < val:
                        need[key] = val
                for key, val in need.items():
                    waited[key] = val
                    if key[0] == "e":
                        si, sv = (val - 1) // SEM_CAP, (val - 1) % SEM_CAP + 1
                        e.wait_ge(esems[key[1]][si], sv)
                    else:
                        e.wait_ge(dsems[key[1]], val)
                if op.fn is None:
                    continue
                ins = op.fn(e)
                if op.dma_key is not None:
                    ins.then_inc(dsems[op.dma_key], 16)
                elif op.sig:
                    n = signum[(ename, op.idx)]
                    ins.then_inc(esems[ename][(n - 1) // SEM_CAP], 1)

        @block.sync
        def _(e):
            replay("sp", e)

        @block.scalar
        def _(e):
            replay("act", e)

        @block.vector
        def _(e):
            replay("dve", e)

        @block.gpsimd
        def _(e):
            replay("pool", e)

        @block.tensor
        def _(e):
            replay("pe", e)


def build_program(debug_stop=None):
    nc = bass.Bass("TRN2", target_bir_lowering=False)
    stack = contextlib.ExitStack()
    P = Prog()

    def din(name, shape, dt=F32):
        return nc.dram_tensor(name, list(shape), dt, kind="ExternalInput").ap()

    x_in = din("x", [S, D])
    smalls_in = din("smalls", [128, NS])
    gbc_in = din("gbc", [128, D])
    dft_in = din("dft", [64, 128, 1024], BF16)
    w_in_in = din("w_in", [D, 1536])
    fmap_in = din("fmap", [4, 128, 128])
    w_out_in = din("w_out", [D, D])
    pmap_in = din("pmap", [4, 256, 256])
    wg_in = din("wg", [2, D, DFF])
    wu_in = din("wu", [2, D, DFF])
    wd_in = din("wd", [2, DFF, D])
    out_d = nc.dram_tensor("out", [S, D], F32, kind="ExternalOutput").ap()

    def sb(name, cols, dt):
        h = stack.enter_context(nc.sbuf_tensor(name, [128, cols], dt))
        return Buf(name, h, dt)

    XTb = sb("XT", NCH * S, F32)
    HTb = sb("HT", NCH * S, BF16)
    RB = 51200
    Rb = sb("R", RB // 2, BF16)
    WBb = sb("WB", 4 * 3072, BF16)
    WDb = sb("WD", 6144, BF16)
    TMPb = sb("TMP", 4096, BF16)
    SMb = sb("SM", NS, F32)
    IDBb = sb("IDB", 128, BF16)
    ONESb = sb("ONES", 128, BF16)
    CSMb = sb("CSM", 1024, BF16)
    FMb = sb("FM", 512, F32)
    EPSb = sb("EPS", 4, F32)
    PSb = []
    for i in range(8):
        h = stack.enter_context(nc.psum_tensor("ps%d" % i, [128, 512], F32))
        PSb.append(Buf("ps%d" % i, h, F32))

    XT = T(XTb, 0, F32)
    HT = T(HTb, 0, BF16)
    SM = T(SMb, 0, F32)
    IDB = T(IDBb, 0, BF16)
    ONES = T(ONESb, 0, BF16)
    CSM = T(CSMb, 0, BF16)
    FM = T(FMb, 0, F32)
    EPS = T(EPSb, 0, F32)
    PS = [T(b, 0, F32) for b in PSb]
    ident = SM(C_ID, 128)

    SQ = [T(TMPb, i * 1024, BF16) for i in range(3)]
    TF = [T(TMPb, 3072, F32), T(TMPb, 5120, F32)]
    TSP = T(TMPb, 7168, F32)

    def mm(out, lhsT, rhs, start, stop):
        return P.add("pe", lambda e: e.matmul(out.ap, lhsT=lhsT.ap, rhs=rhs.ap, start=start, stop=stop),
                     reads=[lhsT, rhs], writes=[out])

    def tr(out, in_):
        return P.add("pe", lambda e: e.transpose(out=out.ap, in_=in_.ap, identity=ident.ap),
                     reads=[in_, ident], writes=[out])

    def copy(eng, out, in_):
        if eng == "act":
            return P.add("act", lambda e: e.activation(out=out.ap, in_=in_.ap, func=AF.Copy), reads=[in_], writes=[out])
        return P.add(eng, lambda e: e.tensor_copy(out=out.ap, in_=in_.ap), reads=[in_], writes=[out])

    def tt_op(eng, out, a, b, op):
        return P.add(eng, lambda e: e.tensor_tensor(out=out.ap, in0=a.ap, in1=b.ap, op=op), reads=[a, b], writes=[out])

    def stt(eng, out, in0, scalar, in1, op0, op1):
        rd = [in0, in1]
        sc = scalar
        if isinstance(scalar, View):
            rd.append(scalar)
            sc = scalar.ap
        return P.add(eng, lambda e: e.scalar_tensor_tensor(out=out.ap, in0=in0.ap, scalar=sc, in1=in1.ap, op0=op0, op1=op1),
                     reads=rd, writes=[out])

    def ts(eng, out, in0, s1, s2, op0, op1=None):
        rd = [in0]
        a1, a2 = s1, s2
        if isinstance(s1, View):
            rd.append(s1)
            a1 = s1.ap
        if isinstance(s2, View):
            rd.append(s2)
            a2 = s2.ap
        if op1 is None:
            return P.add(eng, lambda e: e.tensor_scalar(out=out.ap, in0=in0.ap, scalar1=a1, scalar2=None, op0=op0),
                         reads=rd, writes=[out])
        return P.add(eng, lambda e: e.tensor_scalar(out=out.ap, in0=in0.ap, scalar1=a1, scalar2=a2, op0=op0, op1=op1),
                     reads=rd, writes=[out])

    def act(out, in_, func, scale=1.0, bias=None):
        rd = [in_]
        sc = scale
        if isinstance(scale, View):
            rd.append(scale)
            sc = scale.ap
        if bias is None:
            return P.add("act", lambda e: e.activation(out=out.ap, in_=in_.ap, func=func, scale=sc), reads=rd, writes=[out])
        rd.append(bias)
        return P.add("act", lambda e: e.activation(out=out.ap, in_=in_.ap, func=func, bias=bias.ap, scale=sc),
                     reads=rd, writes=[out])

    def memset(eng, out, val):
        return P.add(eng, lambda e: e.memset(out.ap, val), writes=[out])

    def dma(eng, out_ap, in_ap, key, reads=(), writes=()):
        return P.add(eng, lambda e: e.dma_start(out=out_ap, in_=in_ap), reads=reads, writes=writes, dma_key=key)

    v = SM(0, NS)
    dma("sp", v.ap, smalls_in, ("sm",), writes=[v])
    v = FM(0, 512)
    dma("sp", v.ap.rearrange("p (h f) -> p h f", h=4), fmap_in.rearrange("h e f -> e h f"), ("fm",), writes=[v])
    memset("dve", ONES(0, 128), 1.0)
    memset("dve", EPS(0, 1), RMS_EPS)
    memset("dve", EPS(1, 1), LN_EPS)
    copy("dve", IDB(0, 128), ident)

    STG = [T(Rb, 0, F32), T(Rb, 16384, F32)]
    xv = x_in.rearrange("(g j p) d -> g p j d", j=4, p=128)
    for g in range(4):
        st = STG[g % 2]
        v = st(0, 4096)
        dma("sp", v.ap.rearrange("p (j d) -> p j d", j=4), xv[g], ("stg", g % 2), writes=[v])
        for c in range(NCH):
            b = P.bank()
            for j in range(4):
                tr(PS[b](j * 128, 128), st(j * 1024 + c * 128, 128))
            copy("act" if c % 2 else "dve", XT(c * S + g * TW, TW), PS[b](0, TW))

    def gain(set_i, c):
        return SM(C_GAIN + set_i * 8 + c, 1)

    def rstd_tile(tt, dst):
        b = P.bank()
        for c in range(NCH):
            sq = SQ[c % 3](0, TW)
            xs = XT(c * S + tt * TW, TW)
            act(sq, xs, AF.Square)
            mm(PS[b](0, TW), ONES(0, 128), sq, c == 0, c == NCH - 1)
        std = TF[0](0, TW)
        act(std, PS[b](0, TW), AF.Sqrt, scale=1.0 / D, bias=EPS(0, 1))
        P.add("dve", lambda e: e.reciprocal(out=dst.ap, in_=std.ap), reads=[std], writes=[dst])

    def norm_tile_to_ht(tt, gset):
        rs = TF[1](0, TW)
        rstd_tile(tt, rs)
        for c in range(NCH):
            stt("dve", HT(c * S + tt * TW, TW), XT(c * S + tt * TW, TW),
                gain(gset, c), rs, ALU.mult, ALU.mult)

    def wblock_dma(dst_view, src2d, col0, ncols, key):
        src = src2d.rearrange("(c p) n -> p c n", p=128)[:, :, col0:col0 + ncols]
        dma("pool", dst_view.ap.rearrange("p (c n) -> p c n", c=NCH), src, key, writes=[dst_view])

    ACTB = T(Rb, 0, BF16)
    WD = T(WDb, 0, BF16)
    WBS = [T(WBb, i * 6144, BF16) for i in range(4)]
    QUARTERS = [(0, 6), (6, 6), (12, 6), (18, 4)]

    def ffn_load_gu(layer, q, hb):
        f0, nf = QUARTERS[q]
        half = nf // 2
        ncols = half * 128
        col0 = (f0 + hb * half) * 128
        wblock_dma(WBS[hb * 2](0, NCH * ncols), wg_in[layer], col0, ncols, ("wb", hb * 2))
        wblock_dma(WBS[hb * 2 + 1](0, NCH * ncols), wu_in[layer], col0, ncols, ("wb", hb * 2 + 1))

    def ffn_load_wd(layer, q):
        f0, nf = QUARTERS[q]
        v = WD(0, nf * 1024)
        src = wd_in[layer][f0 * 128:(f0 + nf) * 128, :].rearrange("(f p) n -> p f n", p=128)
        dma("pool", v.ap.rearrange("p (f n) -> p f n", f=nf), src, ("wd",), writes=[v])

    def ffn(layer, pre_last_q=None, hook=None):
        for q, (f0, nf) in enumerate(QUARTERS):
            half = nf // 2
            ncols = half * 128
            if q == 3 and pre_last_q is not None:
                pre_last_q()
            for hb in range(2):
                wgv, wuv = WBS[hb * 2], WBS[hb * 2 + 1]
                for fl in range(half):
                    fq = hb * half + fl
                    for tt in range(NTT):
                        bg, bu = P.bank(), P.bank()
                        for c in range(NCH):
                            mm(PS[bg](0, TW), wgv(c * ncols + fl * 128, 128), HT(c * S + tt * TW, TW), c == 0, c == NCH - 1)
                        for c in range(NCH):
                            mm(PS[bu](0, TW), wuv(c * ncols + fl * 128, 128), HT(c * S + tt * TW, TW), c == 0, c == NCH - 1)
                        sg = SQ[(fq * NTT + tt) % 3](0, TW)
                        act(sg, PS[bg](0, TW), AF.Silu)
                        tt_op("dve", ACTB(fq * S + tt * TW, TW), PS[bu](0, TW), sg, ALU.mult)
                if q + 1 < 4:
                    ffn_load_gu(layer, q + 1, hb)
            for tt in range(NTT):
                for dc in range(NCH):
                    b = P.bank()
                    for fl in range(nf):
                        mm(PS[b](0, TW), WD(fl * 1024 + dc * 128, 128), ACTB(fl * S + tt * TW, TW), fl == 0, fl == nf - 1)
                    xs = XT(dc * S + tt * TW, TW)
                    tt_op("dve", xs, xs, PS[b](0, TW), ALU.add)
                if q == 3 and hook is not None:
                    hook(tt)
            if q + 1 < 4:
                ffn_load_wd(layer, q + 1)

    for tt in range(NTT):
        norm_tile_to_ht(tt, 0)
    chunk_cols = [0, 128, 256, 384]
    for j in range(4):
        chunk_cols += [512 + 128 * j, 1024 + 128 * j]
    WSL = [T(WBb, i * 2048, BF16) for i in range(12)]
    for k, col0 in enumerate(chunk_cols):
        wblock_dma(WSL[k](0, 1024), w_in_in, col0, 128, ("wsl", k))
    A = T(Rb, 0, BF16)
    U = T(Rb, 16384, BF16)
    UW = S + 30
    YB = T(Rb, 33024, BF16)
    for j in range(4):
        memset("pool", U(j * UW, 15), 0.0)
        memset("pool", U(j * UW + 15 + S, 15), 0.0)
    n_ev = 0
    for j in range(4):
        for tt in range(NTT):
            b = P.bank()
            for c in range(NCH):
                mm(PS[b](0, TW), WSL[j](c * 128, 128), HT(c * S + tt * TW, TW), c == 0, c == NCH - 1)
            copy("act" if n_ev % 2 else "dve", A(j * S + tt * TW, TW), PS[b](0, TW))
            n_ev += 1
    for j in range(4):
        for tt in range(NTT):
            bv, bg = P.bank(), P.bank()
            for c in range(NCH):
                mm(PS[bv](0, TW), WSL[4 + 2 * j](c * 128, 128), HT(c * S + tt * TW, TW), c == 0, c == NCH - 1)
            for c in range(NCH):
                mm(PS[bg](0, TW), WSL[5 + 2 * j](c * 128, 128), HT(c * S + tt * TW, TW), c == 0, c == NCH - 1)
            sig = TF[(j * NTT + tt) % 2](0, TW)
            act(sig, PS[bg](0, TW), AF.Sigmoid)
            tt_op("dve", U(j * UW + 15 + tt * TW, TW), PS[bv](0, TW), sig, ALU.mult)

    WOB = [(0, 384), (384, 384), (768, 256)]
    for k, (c0, ncl) in enumerate(WOB):
        wblock_dma(WBS[k](0, NCH * ncl), w_out_in, c0, ncl, ("wb", k))

    WDS = [T(WDb, i * 2048, BF16) for i in range(6)]
    dft_state = {"next": 0}

    def dft_prefetch(upto):
        while dft_state["next"] < min(upto, 64):
            i = dft_state["next"]
            v = WDS[i % 6](0, 1024)
            dma("sp", v.ap, dft_in[i], ("wds", i % 6), writes=[v])
            dft_state["next"] += 1

    dft_prefetch(6)

    DS = [T(HTb, 0, BF16), T(HTb, 7936, BF16)]
    CT0 = 15872

    def cset(k):
        base = CT0 + k * 5120
        return dict(cb=T(HTb, base, F32), cbb=T(HTb, base + 2048, BF16), rs=T(HTb, base + 3072, F32))

    CS = [cset(k) for k in range(3)]
    tiles = [(j, tt) for j in range(4) for tt in range(NTT)]
    pbank = {}

    def conv_s0(i):
        j, tt = tiles[i]
        if tt == 0:
            for t in range(31):
                P.add("pool", (lambda e, t=t, j=j: e.tensor_scalar(out=DS[j % 2](t * 128, 128).ap, in0=IDB(0, 128).ap,
                                                                    scalar1=SM(C_CW + j * 31 + t, 1).ap, scalar2=None,
                                                                    op0=ALU.mult)),
                      reads=[IDB(0, 128), SM(C_CW + j * 31 + t, 1)], writes=[DS[j % 2](t * 128, 128)])
        b = P.bank()
        for t in range(31):
            mm(PS[b](0, TW), DS[j % 2](t * 128, 128), U(j * UW + tt * TW + t, TW), t == 0, t == 30)
        cs = CS[i % 3]
        ts("dve", cs["cb"](0, TW), PS[b](0, TW), SM(C_CB + j, 1), None, ALU.add)
        copy("pool", cs["cbb"](0, TW), cs["cb"](0, TW))

    def conv_s2(i):
        cs = CS[i % 3]
        b = P.bank()
        mm(PS[b](0, TW), ONES(0, 128), cs["cbb"](0, TW), True, True)
        stt("dve", cs["cb"](0, TW), PS[b](0, TW), -1.0 / 128, cs["cb"](0, TW), ALU.mult, ALU.add)
        tt_op("pool", cs["cbb"](0, TW), cs["cb"](0, TW), cs["cb"](0, TW), ALU.mult)

    def conv_s3(i):
        j, tt = tiles[i]
        cs = CS[i % 3]
        b = P.bank()
        mm(PS[b](0, TW), ONES(0, 128), cs["cbb"](0, TW), True, True)
        act(cs["rs"](0, TW), PS[b](0, TW), AF.Sqrt, scale=1.0 / 128, bias=EPS(1, 1))
        P.add("dve", lambda e: e.reciprocal(out=cs["rs"](0, TW).ap, in_=cs["rs"](0, TW).ap),
              reads=[cs["rs"](0, TW)], writes=[cs["rs"](0, TW)])
        tt_op("dve", cs["cb"](0, TW), cs["cb"](0, TW), cs["rs"](0, TW), ALU.mult)
        act(YB(j * S + tt * TW, TW), cs["cb"](0, TW), AF.Silu, scale=SM(C_LG + j, 1), bias=SM(C_LB + j, 1))

    for i in range(16 + 2):
        if i < 16:
            conv_s0(i)
        if 0 <= i - 1 < 16:
            conv_s2(i - 1)
        if 0 <= i - 2 < 16:
            conv_s3(i - 2)

    for h in range(4):
        b = P.bank()
        mm(PS[b](0, 128), SM(C_CD, 128), FM(h * 128, 128), True, True)
        mm(PS[b](128, 128), SM(C_SD, 128), FM(h * 128, 128), True, True)
        copy("dve", CSM(h * 256, 256), PS[b](0, 256))
    Y = T(HTb, 0, BF16)
    n_ev = 0
    for nt in range(16):
        for hp in range(2):
            b = P.bank()
            for hh in range(2):
                h = 2 * hp + hh
                mm(PS[b](hh * 256, 256), A(h * S + nt * 128, 128), CSM(h * 256, 256), True, True)
            copy("act" if n_ev % 2 else "dve", Y(nt * 1024 + hp * 512, 512), PS[b](0, 512))
            n_ev += 1
    YA = A
    for kt in range(4):
        banks = [P.bank() for _ in range(4)]
        for nt in range(16):
            i = kt * 16 + nt
            dft_prefetch(i + 6)
            tab = WDS[i % 6]
            for h in range(4):
                mm(PS[banks[h]](0, TW), Y(nt * 1024 + h * 256, 128), tab(0, 512), nt == 0, False)
                mm(PS[banks[h]](0, TW), Y(nt * 1024 + h * 256 + 128, 128), tab(512, 512), False, nt == 15)
        for h in range(4):
            copy("act" if h % 2 else "dve", YA(h * S + kt * TW, TW), PS[banks[h]](0, TW))

    ffn_load_wd(0, 0)

    for dc in range(NCH):
        blk = (dc * 128) // 384
        off = (dc * 128) % 384
        ncl = WOB[blk][1]
        for tt in range(NTT):
            b = P.bank()
            for kc in range(NCH):
                rhs = YA(kc * S + tt * TW, TW) if kc < 4 else YB((kc - 4) * S + tt * TW, TW)
                mm(PS[b](0, TW), WBS[blk](kc * ncl + off, 128), rhs, kc == 0, kc == NCH - 1)
            xs = XT(dc * S + tt * TW, TW)
            tt_op("dve", xs, xs, PS[b](0, TW), ALU.add)

    if debug_stop == "mix0":
        return finish_debug(nc, stack, P, XT, out_d)

    ffn_load_gu(0, 0, 0)
    ffn_load_gu(0, 0, 1)
    for tt in range(NTT):
        norm_tile_to_ht(tt, 1)

    RSTD = T(Rb, 24704, F32)
    PM = T(WDb, 8192, BF16)

    def load_pm():
        v = PM(0, 2048)
        dma("pool", v.ap.rearrange("p (a n) -> p a n", a=8), pmap_in.rearrange("g (kc p) n -> p (g kc) n", p=128),
            ("pm",), writes=[v])

    def pool_norm_hook(tt):
        rstd_tile(tt, RSTD(tt * TW, TW))

    ffn(0, pre_last_q=load_pm, hook=pool_norm_hook)

    if debug_stop == "ffn0":
        return finish_debug(nc, stack, P, XT, out_d)

    ffn_load_gu(1, 0, 0)
    ffn_load_gu(1, 0, 1)
    HW_ = S + 16
    T1 = T(Rb, 0, F32)
    T2 = T(Rb, 8256, F32)
    PG = T(Rb, 16512, BF16)
    HH = [T(Rb, 32896, F32), T(Rb, 41152, F32)]
    ET = TSP
    for k in range(2):
        memset("pool", HH[k](0, 8), 0.0)
        memset("pool", HH[k](8 + S, 8), 0.0)
    def pieces(n, step=512):
        return [(o, min(step, n - o)) for o in range(0, n, step)]

    for c in range(NCH):
        gi = c // 2
        cc = c % 2
        w = 2 << gi
        half = w // 2
        H = HH[c % 2]
        for (o, n) in pieces(S):
            tt_op("pool", H(8 + o, n), XT(c * S + o, n), RSTD(o, n), ALU.mult)
            ts("pool", H(8 + o, n), H(8 + o, n), gain(2, c), None, ALU.mult)
        for (o, n) in pieces(2063):
            tt_op("dve", T1(o, n), H(o, n), H(1 + o, n), ALU.add)
        Bw = T1
        if w >= 4:
            for (o, n) in pieces(2061):
                tt_op("dve", T2(o, n), T1(o, n), T1(2 + o, n), ALU.add)
            Bw = T2
        if w >= 8:
            for (o, n) in pieces(2057):
                tt_op("dve", T1(o, n), T2(o, n), T2(4 + o, n), ALU.add)
            Bw = T1
        if w >= 16:
            for (o, n) in pieces(2049):
                tt_op("dve", T2(o, n), T1(o, n), T1(8 + o, n), ALU.add)
            Bw = T2
        for (o, n) in pieces(S):
            stt("dve", PG(cc * S + o, n), Bw(8 - half + o, n), 1.0 / w, H(8 + o, n), ALU.mult, ALU.subtract)
        tt_op("dve", ET(0, 8), Bw(8 - half, 8), SM(C_IC + gi * 16, 8), ALU.mult)
        tt_op("dve", ET(8, 8), Bw(8 - half + S - 8, 8), SM(C_IC + gi * 16 + 8, 8), ALU.mult)
        tt_op("dve", PG(cc * S, 8), ET(0, 8), H(8, 8), ALU.subtract)
        tt_op("dve", PG(cc * S + S - 8, 8), ET(8, 8), H(8 + S - 8, 8), ALU.subtract)
        if cc == 1:
            for dcl in range(2):
                co = gi * 2 + dcl
                for tt in range(NTT):
                    b = P.bank()
                    for kc in range(2):
                        mm(PS[b](0, TW), PM((gi * 2 + kc) * 256 + dcl * 128, 128), PG(kc * S + tt * TW, TW), kc == 0, kc == 1)
                    xs = XT(co * S + tt * TW, TW)
                    stt("dve", xs, PS[b](0, TW), SM(C_PS + co, 1), xs, ALU.mult, ALU.add)

    if debug_stop == "mix1":
        return finish_debug(nc, stack, P, XT, out_d)

    ffn_load_wd(1, 0)
    for tt in range(NTT):
        norm_tile_to_ht(tt, 3)

    GB = T(Rb, 24576, F32)
    OUTT = [T(Rb, 36864, F32), T(Rb, 40960, F32)]
    ov = out_d.rearrange("(t p) d -> t p d", p=128)
    out_toks = []

    def load_gb():
        v = GB(0, D)
        dma("sp", v.ap, gbc_in, ("gb",), writes=[v])

    def final_hook(tt):
        rs = TF[1](0, TW)
        rstd_tile(tt, rs)
        for c in range(NCH):
            xs = XT(c * S + tt * TW, TW)
            tt_op("pool", xs, xs, rs, ALU.mult)
        for t4 in range(4):
            ti = tt * 4 + t4
            k = ti % 2
            for hf in range(2):
                b = P.bank()
                for cc in range(4):
                    c = hf * 4 + cc
                    tr(PS[b](cc * 128, 128), XT(c * S + ti * 128, 128))
                tt_op("dve", OUTT[k](hf * TW, TW), PS[b](0, TW), GB(hf * TW, TW), ALU.mult)
            o = OUTT[k](0, D)
            out_toks.append(dma("sp", ov[ti], o.ap, ("out", k), reads=[o]))

    if debug_stop == "ffn1":
        ffn(1)
        return finish_debug(nc, stack, P, XT, out_d)
    ffn(1, pre_last_q=load_gb, hook=final_hook)
    P.add("sp", None, extra_deps=out_toks)
    P.emit(nc, stack)
    stack.close()
    return nc


def finish_debug(nc, stack, P, XT, out_d):
    toks = []
    ov = out_d.rearrange("(c p) (a n) -> c p (a n)", p=128, a=1)
    ov2 = out_d.rearrange("(c h p) n -> c h p n", c=8, h=2)
    for c in range(NCH):
        for h in range(2):
            v = XT(c * S + h * 1024, 1024)
            toks.append(P.add("sp", (lambda e, v=v, c=c, h=h: e.dma_start(out=ov2[c, h], in_=v.ap)), reads=[v],
                              dma_key=("out", (c * 2 + h) % 2)))
    P.add("sp", None, extra_deps=toks)
    P.emit(nc, stack)
    stack.close()
    return nc


def _consts():
    n = np.arange(S, dtype=np.int64)
    ang = 2.0 * np.pi * ((n[:, None] * n[None, :]) % S).astype(np.float64) / S
    Cs = np.cos(ang)
    Sn = -np.sin(ang)
    tab = np.empty((4, 16, 128, 2, 512), dtype=np.float32)
    for kt in range(4):
        for nt in range(16):
            tab[kt, nt, :, 0, :] = Cs[nt * 128:(nt + 1) * 128, kt * 512:(kt + 1) * 512]
            tab[kt, nt, :, 1, :] = Sn[nt * 128:(nt + 1) * 128, kt * 512:(kt + 1) * 512]
    tab = tab.reshape(64, 128, 1024).astype(ml_dtypes.bfloat16)
    e = np.arange(128, dtype=np.int64)
    ang = 2.0 * np.pi * ((e[:, None] * e[None, :]) % 128).astype(np.float64) / 128
    Cd = (np.cos(ang) / 512.0).astype(np.float32)
    Sd = (np.sin(ang) / 512.0).astype(np.float32)
    ic = np.zeros((4, 16), dtype=np.float32)
    for gi in range(4):
        w = 2 << gi
        for i in range(16):
            pos = i if i < 8 else S - 16 + i
            lo = min(max(pos - w // 2, 0), S)
            hi = min(max(pos - w // 2 + w, 0), S)
            ic[gi, i] = 1.0 / float(hi - lo)
    return tab, Cd, Sd, ic


def _prep(inputs):
    tab, Cd, Sd, ic = _consts()
    f = lambda a: np.ascontiguousarray(np.asarray(a, dtype=np.float32))
    sm = np.zeros((128, NS), dtype=np.float32)
    sm[:, C_ID:C_ID + 128] = np.eye(128, dtype=np.float32)
    sm[:, C_CD:C_CD + 128] = Cd
    sm[:, C_SD:C_SD + 128] = Sd
    nm, nf_ = f(inputs["norm_mix_g"]), f(inputs["norm_ffn_g"])
    for si, g in enumerate([nm[0], nf_[0], nm[1], nf_[1]]):
        sm[:, C_GAIN + si * 8:C_GAIN + si * 8 + 8] = g.reshape(8, 128).T
    cw = f(inputs["conv_w"])[0]
    sm[:, C_CW:C_CW + 124] = cw.reshape(31, 4, 128).transpose(2, 1, 0).reshape(128, 124)
    sm[:, C_CB:C_CB + 4] = f(inputs["conv_b"])[0].reshape(4, 128).T
    sm[:, C_LG:C_LG + 4] = f(inputs["conv_ln_g"])[0].reshape(4, 128).T
    sm[:, C_LB:C_LB + 4] = f(inputs["conv_ln_b"])[0].reshape(4, 128).T
    sm[:, C_PS:C_PS + 8] = f(inputs["pool_scale"])[0].reshape(8, 128).T
    sm[:, C_IC:C_IC + 64] = np.broadcast_to(ic.reshape(1, 64), (128, 64))
    gbc = np.ascontiguousarray(np.broadcast_to(f(inputs["final_g"]).reshape(1, D), (128, D)))
    shared = dict(
        smalls=sm, gbc=gbc, dft=tab,
        w_in=f(inputs["w_in_ab"])[0], fmap=f(inputs["fnet_map"])[0], w_out=f(inputs["w_out_ab"])[0],
        pmap=f(inputs["pool_map"])[0], wg=f(inputs["ffn_w_gate"]), wu=f(inputs["ffn_w_up"]), wd=f(inputs["ffn_w_down"]),
    )
    x = f(inputs["x"])
    return [dict(shared, x=x[b]) for b in range(8)]


_NC_CACHE = {}


def kernel(**inputs):
    in_maps = _prep(inputs)
    if "nc" not in _NC_CACHE:
        _NC_CACHE["nc"] = build_program()
    nc = _NC_CACHE["nc"]
    res = run_bass_kernel_spmd(nc, in_maps, core_ids=list(range(8)))
    return np.stack([np.asarray(r["out"], dtype=np.float32) for r in res.results], axis=0)
```

```python
import contextlib
import numpy as np
import ml_dtypes
import concourse.bass as bass
import concourse.mybir as mybir
from concourse.bass_utils import run_bass_kernel_spmd

F32 = mybir.dt.float32
BF16 = mybir.dt.bfloat16
AF = mybir.ActivationFunctionType
ALU = mybir.AluOpType
AX = mybir.AxisListType

S = 2048
D = 1024
NCH = 8
TW = 512
NTT = 4
DFF = 2816
NF = 22
RMS_EPS = 1e-6
LN_EPS = 1e-5
ESZ = {F32: 4, BF16: 2}

C_ID = 0
C_CD = 128
C_SD = 256
C_GAIN = 384
C_CW = 416
C_CB = 540
C_LG = 544
C_LB = 548
C_PS = 552
C_IC = 560
NS = 624

GRAN = 256
SEM_CAP = 3000


class Buf:
    def __init__(self, name, handle, base_dt):
        self.name = name
        self.h = handle
        self.base_dt = base_dt
        self.besz = ESZ[base_dt]
        self.state = {}


class View:
    __slots__ = ("ap", "buf", "lo", "hi")

    def __init__(self, ap, buf, lo, hi):
        self.ap, self.buf, self.lo, self.hi = ap, buf, lo, hi


class T:
    def __init__(self, buf, off, dt):
        self.buf, self.off, self.dt, self.esz = buf, off, dt, ESZ[dt]

    def __call__(self, lo, n):
        b0 = self.off + lo * self.esz
        b1 = b0 + n * self.esz
        bes = self.buf.besz
        assert b0 % bes == 0 and b1 % bes == 0, (self.buf.name, b0, b1)
        ap = self.buf.h[:, b0 // bes:b1 // bes]
        if self.dt != self.buf.base_dt:
            assert b0 % 4 == 0 and b1 % 4 == 0
            ap = ap.bitcast(self.dt)
        return View(ap, self.buf, b0, b1)


class Op:
    __slots__ = ("eng", "fn", "deps", "dma_key", "dma_val", "idx", "sig")


class Prog:
    ENGS = ("pe", "act", "dve", "pool", "sp")

    def __init__(self):
        self.ops = {e: [] for e in self.ENGS}
        self.dma_cnt = {}
        self.nbank = 0

    def _tok(self, op):
        if op.dma_key is not None:
            return ("d", op.dma_key, op.dma_val)
        return ("e", op.eng, op.idx)

    def add(self, eng, fn, reads=(), writes=(), dma_key=None, extra_deps=()):
        op = Op()
        op.eng, op.fn, op.dma_key, op.sig = eng, fn, dma_key, False
        op.idx = len(self.ops[eng])
        op.dma_val = None
        if dma_key is not None:
            self.dma_cnt[dma_key] = self.dma_cnt.get(dma_key, 0) + 16
            op.dma_val = self.dma_cnt[dma_key]
        tok = self._tok(op)
        deps = set(extra_deps)
        for v in reads:
            st = v.buf.state
            for g in range(v.lo // GRAN, (v.hi - 1) // GRAN + 1):
                ent = st.get(g)
                if ent is None:
                    ent = st[g] = [None, []]
                if ent[0] is not None:
                    deps.add(ent[0])
                ent[1].append(tok)
        for v in writes:
            st = v.buf.state
            for g in range(v.lo // GRAN, (v.hi - 1) // GRAN + 1):
                ent = st.get(g)
                if ent is None:
                    ent = st[g] = [None, []]
                if ent[0] is not None:
                    deps.add(ent[0])
                for r in ent[1]:
                    deps.add(r)
                ent[0] = tok
                ent[1] = []
        deps.discard(tok)
        op.deps = [d for d in deps if not (d[0] == "e" and d[1] == eng)]
        self.ops[eng].append(op)
        return tok

    def bank(self):
        b = self.nbank % 8
        self.nbank += 1
        return b

    def emit(self, nc, stack):
        for e in self.ENGS:
            for op in self.ops[e]:
                for d in op.deps:
                    if d[0] == "e":
                        self.ops[d[1]][d[2]].sig = True
        signum = {}
        nsig = {}
        for e in self.ENGS:
            n = 0
            for op in self.ops[e]:
                if op.sig and op.dma_key is None:
                    n += 1
                    signum[(e, op.idx)] = n
            nsig[e] = n
        esems = {}
        for e in self.ENGS:
            k = (nsig[e] + SEM_CAP - 1) // SEM_CAP
            esems[e] = [stack.enter_context(nc.semaphore("p_%s_%d" % (e, i))) for i in range(max(k, 1))]
        dsems = {}
        for key in self.dma_cnt:
            dsems[key] = stack.enter_context(nc.semaphore("d_" + "_".join(str(x) for x in key)))
        block = stack.enter_context(nc.Block())

        def replay(ename, e):
            waited = {}
            for op in self.ops[ename]:
                need = {}
                for d in op.deps:
                    if d[0] == "e":
                        n = signum[(d[1], d[2])]
                        key = ("e", d[1])
                        val = n
                    else:
                        key = ("d", d[1])
                        val = d[2]
                    if waited.get(key, 0) >= val:
                        continue
                    if need.get(key, 0) < val:
                        need[key] = val
                for key, val in need.items():
                    waited[key] = val
                    if key[0] == "e":
                        si, sv = (val - 1) // SEM_CAP, (val - 1) % SEM_CAP + 1
                        e.wait_ge(esems[key[1]][si], sv)
                    else:
                        e.wait_ge(dsems[key[1]], val)
                if op.fn is None:
                    continue
                ins = op.fn(e)
                if op.dma_key is not None:
                    ins.then_inc(dsems[op.dma_key], 16)
                elif op.sig:
                    n = signum[(ename, op.idx)]
                    ins.then_inc(esems[ename][(n - 1) // SEM_CAP], 1)

        @block.sync
        def _(e):
            replay("sp", e)

        @block.scalar
        def _(e):
            replay("act", e)

        @block.vector
        def _(e):
            replay("dve", e)

        @block.gpsimd
        def _(e):
            replay("pool", e)

        @block.tensor
        def _(e):
            replay("pe", e)


def build_program(debug_stop=None):
    nc = bass.Bass("TRN2", target_bir_lowering=False)
    stack = contextlib.ExitStack()
    P = Prog()

    def din(name, shape, dt=F32):
        return nc.dram_tensor(name, list(shape), dt, kind="ExternalInput").ap()

    x_in = din("x", [S, D])
    smalls_in = din("smalls", [128, NS])
    gbc_in = din("gbc", [128, D])
    dft_in = din("dft", [64, 128, 1024], BF16)
    w_in_in = din("w_in", [D, 1536])
    fmap_in = din("fmap", [4, 128, 128])
    w_out_in = din("w_out", [D, D])
    pmap_in = din("pmap", [4, 256, 256])
    wg_in = din("wg", [2, D, DFF])
    wu_in = din("wu", [2, D, DFF])
    wd_in = din("wd", [2, DFF, D])
    out_d = nc.dram_tensor("out", [S, D], F32, kind="ExternalOutput").ap()

    def sb(name, cols, dt):
        h = stack.enter_context(nc.sbuf_tensor(name, [128, cols], dt))
        return Buf(name, h, dt)

    XTb = sb("XT", NCH * S, F32)
    HTb = sb("HT", NCH * S, BF16)
    RB = 51200
    Rb = sb("R", RB // 2, BF16)
    WBb = sb("WB", 4 * 3072, BF16)
    WDb = sb("WD", 6144, BF16)
    TMPb = sb("TMP", 4096, BF16)
    SMb = sb("SM", NS, F32)
    IDBb = sb("IDB", 128, BF16)
    ONESb = sb("ONES", 128, BF16)
    CSMb = sb("CSM", 1024, BF16)
    FMb = sb("FM", 512, F32)
    EPSb = sb("EPS", 4, F32)
    PSb = []
    for i in range(8):
        h = stack.enter_context(nc.psum_tensor("ps%d" % i, [128, 512], F32))
        PSb.append(Buf("ps%d" % i, h, F32))

    XT = T(XTb, 0, F32)
    HT = T(HTb, 0, BF16)
    SM = T(SMb, 0, F32)
    IDB = T(IDBb, 0, BF16)
    ONES = T(ONESb, 0, BF16)
    CSM = T(CSMb, 0, BF16)
    FM = T(FMb, 0, F32)
    EPS = T(EPSb, 0, F32)
    PS = [T(b, 0, F32) for b in PSb]
    ident = SM(C_ID, 128)

    SQ = [T(TMPb, i * 1024, BF16) for i in range(3)]
    TF = [T(TMPb, 3072, F32), T(TMPb, 5120, F32)]
    TSP = T(TMPb, 7168, F32)

    def mm(out, lhsT, rhs, start, stop):
        return P.add("pe", lambda e: e.matmul(out.ap, lhsT=lhsT.ap, rhs=rhs.ap, start=start, stop=stop),
                     reads=[lhsT, rhs], writes=[out])

    def tr(out, in_):
        return P.add("pe", lambda e: e.transpose(out=out.ap, in_=in_.ap, identity=ident.ap),
                     reads=[in_, ident], writes=[out])

    def copy(eng, out, in_):
        if eng == "act":
            return P.add("act", lambda e: e.activation(out=out.ap, in_=in_.ap, func=AF.Copy), reads=[in_], writes=[out])
        return P.add(eng, lambda e: e.tensor_copy(out=out.ap, in_=in_.ap), reads=[in_], writes=[out])

    def tt_op(eng, out, a, b, op):
        return P.add(eng, lambda e: e.tensor_tensor(out=out.ap, in0=a.ap, in1=b.ap, op=op), reads=[a, b], writes=[out])

    def stt(eng, out, in0, scalar, in1, op0, op1):
        rd = [in0, in1]
        sc = scalar
        if isinstance(scalar, View):
            rd.append(scalar)
            sc = scalar.ap
        return P.add(eng, lambda e: e.scalar_tensor_tensor(out=out.ap, in0=in0.ap, scalar=sc, in1=in1.ap, op0=op0, op1=op1),
                     reads=rd, writes=[out])

    def ts(eng, out, in0, s1, s2, op0, op1=None):
        rd = [in0]
        a1, a2 = s1, s2
        if isinstance(s1, View):
            rd.append(s1)
            a1 = s1.ap
        if isinstance(s2, View):
            rd.append(s2)
            a2 = s2.ap
        if op1 is None:
            return P.add(eng, lambda e: e.tensor_scalar(out=out.ap, in0=in0.ap, scalar1=a1, scalar2=None, op0=op0),
                         reads=rd, writes=[out])
        return P.add(eng, lambda e: e.tensor_scalar(out=out.ap, in0=in0.ap, scalar1=a1, scalar2=a2, op0=op0, op1=op1),
                     reads=rd, writes=[out])

    def act(out, in_, func, scale=1.0, bias=None):
        rd = [in_]
        sc = scale
        if isinstance(scale, View):
            rd.append(scale)
            sc = scale.ap
        if bias is None:
            return P.add("act", lambda e: e.activation(out=out.ap, in_=in_.ap, func=func, scale=sc), reads=rd, writes=[out])
        rd.append(bias)
        return P.add("act", lambda e: e.activation(out=out.ap, in_=in_.ap, func=func, bias=bias.ap, scale=sc),
                     reads=rd, writes=[out])

    def memset(eng, out, val):
        return P.add(eng, lambda e: e.memset(out.ap, val), writes=[out])

    def dma(eng, out_ap, in_ap, key, reads=(), writes=()):
        return P.add(eng, lambda e: e.dma_start(out=out_ap, in_=in_ap), reads=reads, writes=writes, dma_key=key)

    v = SM(0, NS)
    dma("sp", v.ap, smalls_in, ("sm",), writes=[v])
    v = FM(0, 512)
    dma("sp", v.ap.rearrange("p (h f) -> p h f", h=4), fmap_in.rearrange("h e f -> e h f"), ("fm",), writes=[v])
    memset("dve", ONES(0, 128), 1.0)
    memset("dve", EPS(0, 1), RMS_EPS)
    memset("dve", EPS(1, 1), LN_EPS)
    copy("dve", IDB(0, 128), ident)

    STG = [T(Rb, 0, F32), T(Rb, 16384, F32)]
    xv = x_in.rearrange("(g j p) d -> g p j d", j=4, p=128)
    for g in range(4):
        st = STG[g % 2]
        v = st(0, 4096)
        dma("sp", v.ap.rearrange("p (j d) -> p j d", j=4), xv[g], ("stg", g % 2), writes=[v])
        for c in range(NCH):
            b = P.bank()
            for j in range(4):
                tr(PS[b](j * 128, 128), st(j * 1024 + c * 128, 128))
            copy("act" if c % 2 else "dve", XT(c * S + g * TW, TW), PS[b](0, TW))

    def gain(set_i, c):
        return SM(C_GAIN + set_i * 8 + c, 1)

    def rstd_tile(tt, dst):
        b = P.bank()
        for c in range(NCH):
            sq = SQ[c % 3](0, TW)
            xs = XT(c * S + tt * TW, TW)
            act(sq, xs, AF.Square)
            mm(PS[b](0, TW), ONES(0, 128), sq, c == 0, c == NCH - 1)
        std = TF[0](0, TW)
        act(std, PS[b](0, TW), AF.Ln, scale=1.0 / D, bias=EPS(0, 1))
        act(dst, std, AF.Exp, scale=-0.5)

    def norm_tile_to_ht(tt, gset):
        rs = TF[1](0, TW)
        rstd_tile(tt, rs)
        for c in range(NCH):
            stt("dve", HT(c * S + tt * TW, TW), XT(c * S + tt * TW, TW),
                gain(gset, c), rs, ALU.mult, ALU.mult)

    def wblock_dma(dst_view, src2d, col0, ncols, key):
        src = src2d.rearrange("(c p) n -> p c n", p=128)[:, :, col0:col0 + ncols]
        dma("pool", dst_view.ap.rearrange("p (c n) -> p c n", c=NCH), src, key, writes=[dst_view])

    ACTB = T(Rb, 0, BF16)
    WD = T(WDb, 0, BF16)
    WBS = [T(WBb, i * 6144, BF16) for i in range(4)]
    QUARTERS = [(0, 6), (6, 6), (12, 6), (18, 4)]

    def ffn_load_gu(layer, q, hb):
        f0, nf = QUARTERS[q]
        half = nf // 2
        ncols = half * 128
        col0 = (f0 + hb * half) * 128
        wblock_dma(WBS[hb * 2](0, NCH * ncols), wg_in[layer], col0, ncols, ("wb", hb * 2))
        wblock_dma(WBS[hb * 2 + 1](0, NCH * ncols), wu_in[layer], col0, ncols, ("wb", hb * 2 + 1))

    def ffn_load_wd(layer, q):
        f0, nf = QUARTERS[q]
        v = WD(0, nf * 1024)
        src = wd_in[layer][f0 * 128:(f0 + nf) * 128, :].rearrange("(f p) n -> p f n", p=128)
        dma("pool", v.ap.rearrange("p (f n) -> p f n", f=nf), src, ("wd",), writes=[v])

    def ffn(layer, pre_last_q=None, hook=None):
        for q, (f0, nf) in enumerate(QUARTERS):
            half = nf // 2
            ncols = half * 128
            if q == 3 and pre_last_q is not None:
                pre_last_q()
            for hb in range(2):
                wgv, wuv = WBS[hb * 2], WBS[hb * 2 + 1]
                for fl in range(half):
                    fq = hb * half + fl
                    for tt in range(NTT):
                        bg, bu = P.bank(), P.bank()
                        for c in range(NCH):
                            mm(PS[bg](0, TW), wgv(c * ncols + fl * 128, 128), HT(c * S + tt * TW, TW), c == 0, c == NCH - 1)
                        for c in range(NCH):
                            mm(PS[bu](0, TW), wuv(c * ncols + fl * 128, 128), HT(c * S + tt * TW, TW), c == 0, c == NCH - 1)
                        sg = SQ[(fq * NTT + tt) % 3](0, TW)
                        act(sg, PS[bg](0, TW), AF.Silu)
                        tt_op("dve", ACTB(fq * S + tt * TW, TW), PS[bu](0, TW), sg, ALU.mult)
                if q + 1 < 4:
                    ffn_load_gu(layer, q + 1, hb)
            for tt in range(NTT):
                for dc in range(NCH):
                    b = P.bank()
                    for fl in range(nf):
                        mm(PS[b](0, TW), WD(fl * 1024 + dc * 128, 128), ACTB(fl * S + tt * TW, TW), fl == 0, fl == nf - 1)
                    xs = XT(dc * S + tt * TW, TW)
                    tt_op("dve", xs, xs, PS[b](0, TW), ALU.add)
                if q == 3 and hook is not None:
                    hook(tt)
            if q + 1 < 4:
                ffn_load_wd(layer, q + 1)

    for tt in range(NTT):
        norm_tile_to_ht(tt, 0)
    chunk_cols = [0, 128, 256, 384]
    for j in range(4):
        chunk_cols += [512 + 128 * j, 1024 + 128 * j]
    WSL = [T(WBb, i * 2048, BF16) for i in range(12)]
    for k, col0 in enumerate(chunk_cols):
        wblock_dma(WSL[k](0, 1024), w_in_in, col0, 128, ("wsl", k))
    A = T(Rb, 0, BF16)
    U = T(Rb, 16384, BF16)
    UW = S + 30
    YB = T(Rb, 33024, BF16)
    for j in range(4):
        memset("pool", U(j * UW, 15), 0.0)
        memset("pool", U(j * UW + 15 + S, 15), 0.0)
    n_ev = 0
    for j in range(4):
        for tt in range(NTT):
            b = P.bank()
            for c in range(NCH):
                mm(PS[b](0, TW), WSL[j](c * 128, 128), HT(c * S + tt * TW, TW), c == 0, c == NCH - 1)
            copy("act" if n_ev % 2 else "dve", A(j * S + tt * TW, TW), PS[b](0, TW))
            n_ev += 1
    for j in range(4):
        for tt in range(NTT):
            bv, bg = P.bank(), P.bank()
            for c in range(NCH):
                mm(PS[bv](0, TW), WSL[4 + 2 * j](c * 128, 128), HT(c * S + tt * TW, TW), c == 0, c == NCH - 1)
            for c in range(NCH):
                mm(PS[bg](0, TW), WSL[5 + 2 * j](c * 128, 128), HT(c * S + tt * TW, TW), c == 0, c == NCH - 1)
            sig = TF[(j * NTT + tt) % 2](0, TW)
            act(sig, PS[bg](0, TW), AF.Sigmoid)
            tt_op("dve", U(j * UW + 15 + tt * TW, TW), PS[bv](0, TW), sig, ALU.mult)

    WOB = [(0, 384), (384, 384), (768, 256)]
    for k, (c0, ncl) in enumerate(WOB):
        wblock_dma(WBS[k](0, NCH * ncl), w_out_in, c0, ncl, ("wb", k))

    WDS = [T(WDb, i * 2048, BF16) for i in range(6)]
    dft_state = {"next": 0}

    def dft_prefetch(upto):
        while dft_state["next"] < min(upto, 64):
            i = dft_state["next"]
            v = WDS[i % 6](0, 1024)
            dma("sp", v.ap, dft_in[i], ("wds", i % 6), writes=[v])
            dft_state["next"] += 1

    dft_prefetch(6)

    DS = [T(HTb, 0, BF16), T(HTb, 7936, BF16)]
    CT0 = 15872

    def cset(k):
        base = CT0 + k * 5120
        return dict(cb=T(HTb, base, F32), cbb=T(HTb, base + 2048, BF16), rs=T(HTb, base + 3072, F32))

    CS = [cset(k) for k in range(3)]
    tiles = [(j, tt) for j in range(4) for tt in range(NTT)]
    pbank = {}

    def build_D(j):
        for t in range(31):
            dst = DS[j % 2](t * 128, 128)
            sc = SM(C_CW + j * 31 + t, 1)
            if t % 2 == 0:
                ts("dve", dst, IDB(0, 128), sc, None, ALU.mult)
            else:
                act(dst, IDB(0, 128), AF.Copy, scale=sc)

    def conv_s0(i):
        j, tt = tiles[i]
        if i == 0:
            build_D(0)
        if tt == 0 and j + 1 < 4:
            build_D(j + 1)
        b = P.bank()
        for t in range(31):
            mm(PS[b](0, TW), DS[j % 2](t * 128, 128), U(j * UW + tt * TW + t, TW), t == 0, t == 30)
        cs = CS[i % 3]
        ts("dve", cs["cb"](0, TW), PS[b](0, TW), SM(C_CB + j, 1), None, ALU.add)
        copy("act", cs["cbb"](0, TW), cs["cb"](0, TW))

    def conv_s2(i):
        cs = CS[i % 3]
        b = P.bank()
        mm(PS[b](0, TW), ONES(0, 128), cs["cbb"](0, TW), True, True)
        stt("dve", cs["cb"](0, TW), PS[b](0, TW), -1.0 / 128, cs["cb"](0, TW), ALU.mult, ALU.add)
        tt_op("dve", cs["cbb"](0, TW), cs["cb"](0, TW), cs["cb"](0, TW), ALU.mult)

    def conv_s3(i):
        j, tt = tiles[i]
        cs = CS[i % 3]
        b = P.bank()
        mm(PS[b](0, TW), ONES(0, 128), cs["cbb"](0, TW), True, True)
        act(cs["rs"](0, TW), PS[b](0, TW), AF.Ln, scale=1.0 / 128, bias=EPS(1, 1))
        act(cs["rs"](0, TW), cs["rs"](0, TW), AF.Exp, scale=-0.5)
        tt_op("dve", cs["cb"](0, TW), cs["cb"](0, TW), cs["rs"](0, TW), ALU.mult)
        act(YB(j * S + tt * TW, TW), cs["cb"](0, TW), AF.Silu, scale=SM(C_LG + j, 1), bias=SM(C_LB + j, 1))

    for i in range(16 + 2):
        if i < 16:
            conv_s0(i)
        if 0 <= i - 1 < 16:
            conv_s2(i - 1)
        if 0 <= i - 2 < 16:
            conv_s3(i - 2)

    for h in range(4):
        b = P.bank()
        mm(PS[b](0, 128), SM(C_CD, 128), FM(h * 128, 128), True, True)
        mm(PS[b](128, 128), SM(C_SD, 128), FM(h * 128, 128), True, True)
        copy("dve", CSM(h * 256, 256), PS[b](0, 256))
    Y = T(HTb, 0, BF16)
    n_ev = 0
    for nt in range(16):
        for hp in range(2):
            b = P.bank()
            for hh in range(2):
                h = 2 * hp + hh
                mm(PS[b](hh * 256, 256), A(h * S + nt * 128, 128), CSM(h * 256, 256), True, True)
            copy("act" if n_ev % 2 else "dve", Y(nt * 1024 + hp * 512, 512), PS[b](0, 512))
            n_ev += 1
    YA = A
    for kt in range(4):
        banks = [P.bank() for _ in range(4)]
        for nt in range(16):
            i = kt * 16 + nt
            dft_prefetch(i + 6)
            tab = WDS[i % 6]
            for h in range(4):
                mm(PS[banks[h]](0, TW), Y(nt * 1024 + h * 256, 128), tab(0, 512), nt == 0, False)
                mm(PS[banks[h]](0, TW), Y(nt * 1024 + h * 256 + 128, 128), tab(512, 512), False, nt == 15)
        for h in range(4):
            copy("act" if h % 2 else "dve", YA(h * S + kt * TW, TW), PS[banks[h]](0, TW))

    ffn_load_wd(0, 0)

    for dc in range(NCH):
        blk = (dc * 128) // 384
        off = (dc * 128) % 384
        ncl = WOB[blk][1]
        for tt in range(NTT):
            b = P.bank()
            for kc in range(NCH):
                rhs = YA(kc * S + tt * TW, TW) if kc < 4 else YB((kc - 4) * S + tt * TW, TW)
                mm(PS[b](0, TW), WBS[blk](kc * ncl + off, 128), rhs, kc == 0, kc == NCH - 1)
            xs = XT(dc * S + tt * TW, TW)
            tt_op("dve", xs, xs, PS[b](0, TW), ALU.add)

    if debug_stop == "mix0":
        return finish_debug(nc, stack, P, XT, out_d)

    ffn_load_gu(0, 0, 0)
    ffn_load_gu(0, 0, 1)
    for tt in range(NTT):
        norm_tile_to_ht(tt, 1)

    RSTD = T(Rb, 24704, F32)
    PM = T(WDb, 8192, BF16)

    def load_pm():
        v = PM(0, 2048)
        dma("pool", v.ap.rearrange("p (a n) -> p a n", a=8), pmap_in.rearrange("g (kc p) n -> p (g kc) n", p=128),
            ("pm",), writes=[v])

    def pool_norm_hook(tt):
        rstd_tile(tt, RSTD(tt * TW, TW))

    ffn(0, pre_last_q=load_pm, hook=pool_norm_hook)

    if debug_stop == "ffn0":
        return finish_debug(nc, stack, P, XT, out_d)

    ffn_load_gu(1, 0, 0)
    ffn_load_gu(1, 0, 1)
    HBW = S + 16
    PG = T(Rb, 0, BF16)
    HB = [T(Rb, 8192, BF16), T(Rb, 8192 + 4160, BF16)]
    ET = TSP
    for k in range(2):
        memset("pool", HB[k](0, 8), 0.0)
        memset("pool", HB[k](8 + S, 8), 0.0)
    for c in range(NCH):
        gi = c // 2
        cc = c % 2
        w = 2 << gi
        half = w // 2
        H = HB[c % 2]
        for tt in range(NTT):
            stt("dve", H(8 + tt * TW, TW), XT(c * S + tt * TW, TW), gain(2, c), RSTD(tt * TW, TW), ALU.mult, ALU.mult)
        for tt in range(NTT):
            b = P.bank()
            for t in range(w):
                mm(PS[b](0, TW), IDB(0, 128), H(8 + tt * TW + t - half, TW), t == 0, t == w - 1)
            stt("dve", PG(cc * S + tt * TW, TW), PS[b](0, TW), 1.0 / w, H(8 + tt * TW, TW), ALU.mult, ALU.subtract)
            if tt == 0:
                tt_op("dve", ET(0, 8), PS[b](0, 8), SM(C_IC + gi * 16, 8), ALU.mult)
                tt_op("pool", PG(cc * S, 8), ET(0, 8), H(8, 8), ALU.subtract)
            if tt == NTT - 1:
                tt_op("dve", ET(8, 8), PS[b](TW - 8, 8), SM(C_IC + gi * 16 + 8, 8), ALU.mult)
                tt_op("pool", PG(cc * S + S - 8, 8), ET(8, 8), H(8 + S - 8, 8), ALU.subtract)
        if cc == 1:
            for dcl in range(2):
                co = gi * 2 + dcl
                for tt in range(NTT):
                    b = P.bank()
                    for kc in range(2):
                        mm(PS[b](0, TW), PM((gi * 2 + kc) * 256 + dcl * 128, 128), PG(kc * S + tt * TW, TW), kc == 0, kc == 1)
                    xs = XT(co * S + tt * TW, TW)
                    stt("dve", xs, PS[b](0, TW), SM(C_PS + co, 1), xs, ALU.mult, ALU.add)

    if debug_stop == "mix1":
        return finish_debug(nc, stack, P, XT, out_d)

    ffn_load_wd(1, 0)
    for tt in range(NTT):
        norm_tile_to_ht(tt, 3)

    GB = T(Rb, 24576, F32)
    OUTT = [T(Rb, 36864, F32), T(Rb, 40960, F32)]
    ov = out_d.rearrange("(t p) d -> t p d", p=128)
    out_toks = []

    def load_gb():
        v = GB(0, D)
        dma("sp", v.ap, gbc_in, ("gb",), writes=[v])

    def final_hook(tt):
        rs = TF[1](0, TW)
        rstd_tile(tt, rs)
        for c in range(NCH):
            xs = XT(c * S + tt * TW, TW)
            tt_op("dve", xs, xs, rs, ALU.mult)
        for t4 in range(4):
            ti = tt * 4 + t4
            k = ti % 2
            for hf in range(2):
                b = P.bank()
                for cc in range(4):
                    c = hf * 4 + cc
                    tr(PS[b](cc * 128, 128), XT(c * S + ti * 128, 128))
                tt_op("dve", OUTT[k](hf * TW, TW), PS[b](0, TW), GB(hf * TW, TW), ALU.mult)
            o = OUTT[k](0, D)
            out_toks.append(dma("sp", ov[ti], o.ap, ("out", k), reads=[o]))

    if debug_stop == "ffn1":
        ffn(1)
        return finish_debug(nc, stack, P, XT, out_d)
    ffn(1, pre_last_q=load_gb, hook=final_hook)
    P.add("sp", None, extra_deps=out_toks)
    P.emit(nc, stack)
    stack.close()
    return nc


def finish_debug(nc, stack, P, XT, out_d):
    toks = []
    ov = out_d.rearrange("(c p) (a n) -> c p (a n)", p=128, a=1)
    ov2 = out_d.rearrange("(c h p) n -> c h p n", c=8, h=2)
    for c in range(NCH):
        for h in range(2):
            v = XT(c * S + h * 1024, 1024)
            toks.append(P.add("sp", (lambda e, v=v, c=c, h=h: e.dma_start(out=ov2[c, h], in_=v.ap)), reads=[v],
                              dma_key=("out", (c * 2 + h) % 2)))
    P.add("sp", None, extra_deps=toks)
    P.emit(nc, stack)
    stack.close()
    return nc


def _consts():
    n = np.arange(S, dtype=np.int64)
    ang = 2.0 * np.pi * ((n[:, None] * n[None, :]) % S).astype(np.float64) / S
    Cs = np.cos(ang)
    Sn = -np.sin(ang)
    tab = np.empty((4, 16, 128, 2, 512), dtype=np.float32)
    for kt in range(4):
        for nt in range(16):
            tab[kt, nt, :, 0, :] = Cs[nt * 128:(nt + 1) * 128, kt * 512:(kt + 1) * 512]
            tab[kt, nt, :, 1, :] = Sn[nt * 128:(nt + 1) * 128, kt * 512:(kt + 1) * 512]
    tab = tab.reshape(64, 128, 1024).astype(ml_dtypes.bfloat16)
    e = np.arange(128, dtype=np.int64)
    ang = 2.0 * np.pi * ((e[:, None] * e[None, :]) % 128).astype(np.float64) / 128
    Cd = (np.cos(ang) / 512.0).astype(np.float32)
    Sd = (np.sin(ang) / 512.0).astype(np.float32)
    ic = np.zeros((4, 16), dtype=np.float32)
    for gi in range(4):
        w = 2 << gi
        for i in range(16):
            pos = i if i < 8 else S - 16 + i
            lo = min(max(pos - w // 2, 0), S)
            hi = min(max(pos - w // 2 + w, 0), S)
            ic[gi, i] = 1.0 / float(hi - lo)
    return tab, Cd, Sd, ic


def _prep(inputs):
    tab, Cd, Sd, ic = _consts()
    f = lambda a: np.ascontiguousarray(np.asarray(a, dtype=np.float32))
    sm = np.zeros((128, NS), dtype=np.float32)
    sm[:, C_ID:C_ID + 128] = np.eye(128, dtype=np.float32)
    sm[:, C_CD:C_CD + 128] = Cd
    sm[:, C_SD:C_SD + 128] = Sd
    nm, nf_ = f(inputs["norm_mix_g"]), f(inputs["norm_ffn_g"])
    for si, g in enumerate([nm[0], nf_[0], nm[1], nf_[1]]):
        sm[:, C_GAIN + si * 8:C_GAIN + si * 8 + 8] = g.reshape(8, 128).T
    cw = f(inputs["conv_w"])[0]
    sm[:, C_CW:C_CW + 124] = cw.reshape(31, 4, 128).transpose(2, 1, 0).reshape(128, 124)
    sm[:, C_CB:C_CB + 4] = f(inputs["conv_b"])[0].reshape(4, 128).T
    sm[:, C_LG:C_LG + 4] = f(inputs["conv_ln_g"])[0].reshape(4, 128).T
    sm[:, C_LB:C_LB + 4] = f(inputs["conv_ln_b"])[0].reshape(4, 128).T
    sm[:, C_PS:C_PS + 8] = f(inputs["pool_scale"])[0].reshape(8, 128).T
    sm[:, C_IC:C_IC + 64] = np.broadcast_to(ic.reshape(1, 64), (128, 64))
    gbc = np.ascontiguousarray(np.broadcast_to(f(inputs["final_g"]).reshape(1, D), (128, D)))
    shared = dict(
        smalls=sm, gbc=gbc, dft=tab,
        w_in=f(inputs["w_in_ab"])[0], fmap=f(inputs["fnet_map"])[0], w_out=f(inputs["w_out_ab"])[0],
        pmap=f(inputs["pool_map"])[0], wg=f(inputs["ffn_w_gate"]), wu=f(inputs["ffn_w_up"]), wd=f(inputs["ffn_w_down"]),
    )
    x = f(inputs["x"])
    return [dict(shared, x=x[b]) for b in range(8)]


_NC_CACHE = {}


def kernel(**inputs):
    in_maps = _prep(inputs)
    if "nc" not in _NC_CACHE:
        _NC_CACHE["nc"] = build_program()
    nc = _NC_CACHE["nc"]
    res = run_bass_kernel_spmd(nc, in_maps, core_ids=list(range(8)))
    return np.stack([np.asarray(r["out"], dtype=np.float32) for r in res.results], axis=0)
```

```python
import contextlib
import numpy as np
import ml_dtypes
import concourse.bass as bass
import concourse.mybir as mybir
from concourse.bass_utils import run_bass_kernel_spmd

F32 = mybir.dt.float32
BF16 = mybir.dt.bfloat16
AF = mybir.ActivationFunctionType
ALU = mybir.AluOpType
AX = mybir.AxisListType

S = 2048
D = 1024
NCH = 8
TW = 512
NTT = 4
DFF = 2816
NF = 22
RMS_EPS = 1e-6
LN_EPS = 1e-5
ESZ = {F32: 4, BF16: 2}

C_ID = 0
C_CD = 128
C_SD = 256
C_GAIN = 384
C_CW = 416
C_CB = 540
C_LG = 544
C_LB = 548
C_PS = 552
C_IC = 560
NS = 624

GRAN = 256
SEM_CAP = 3000


class Buf:
    def __init__(self, name, handle, base_dt):
        self.name = name
        self.h = handle
        self.base_dt = base_dt
        self.besz = ESZ[base_dt]
        self.state = {}


class View:
    __slots__ = ("ap", "buf", "lo", "hi")

    def __init__(self, ap, buf, lo, hi):
        self.ap, self.buf, self.lo, self.hi = ap, buf, lo, hi


class T:
    def __init__(self, buf, off, dt):
        self.buf, self.off, self.dt, self.esz = buf, off, dt, ESZ[dt]

    def __call__(self, lo, n):
        b0 = self.off + lo * self.esz
        b1 = b0 + n * self.esz
        bes = self.buf.besz
        assert b0 % bes == 0 and b1 % bes == 0, (self.buf.name, b0, b1)
        ap = self.buf.h[:, b0 // bes:b1 // bes]
        if self.dt != self.buf.base_dt:
            assert b0 % 4 == 0 and b1 % 4 == 0
            ap = ap.bitcast(self.dt)
        return View(ap, self.buf, b0, b1)


class Op:
    __slots__ = ("eng", "fn", "deps", "dma_key", "dma_val", "idx", "sig")


class Prog:
    ENGS = ("pe", "act", "dve", "pool", "sp")

    def __init__(self):
        self.ops = {e: [] for e in self.ENGS}
        self.dma_cnt = {}
        self.nbank = 0

    def _tok(self, op):
        if op.dma_key is not None:
            return ("d", op.dma_key, op.dma_val)
        return ("e", op.eng, op.idx)

    def add(self, eng, fn, reads=(), writes=(), dma_key=None, extra_deps=()):
        op = Op()
        op.eng, op.fn, op.dma_key, op.sig = eng, fn, dma_key, False
        op.idx = len(self.ops[eng])
        op.dma_val = None
        if dma_key is not None:
            self.dma_cnt[dma_key] = self.dma_cnt.get(dma_key, 0) + 16
            op.dma_val = self.dma_cnt[dma_key]
        tok = self._tok(op)
        deps = set(extra_deps)
        for v in reads:
            st = v.buf.state
            for g in range(v.lo // GRAN, (v.hi - 1) // GRAN + 1):
                ent = st.get(g)
                if ent is None:
                    ent = st[g] = [None, []]
                if ent[0] is not None:
                    deps.add(ent[0])
                ent[1].append(tok)
        for v in writes:
            st = v.buf.state
            for g in range(v.lo // GRAN, (v.hi - 1) // GRAN + 1):
                ent = st.get(g)
                if ent is None:
                    ent = st[g] = [None, []]
                if ent[0] is not None:
                    deps.add(ent[0])
                for r in ent[1]:
                    deps.add(r)
                ent[0] = tok
                ent[1] = []
        deps.discard(tok)
        op.deps = [d for d in deps if not (d[0] == "e" and d[1] == eng)]
        self.ops[eng].append(op)
        return tok

    def bank(self):
        b = self.nbank % 8
        self.nbank += 1
        return b

    def emit(self, nc, stack):
        for e in self.ENGS:
            for op in self.ops[e]:
                for d in op.deps:
                    if d[0] == "e":
                        self.ops[d[1]][d[2]].sig = True
        signum = {}
        nsig = {}
        for e in self.ENGS:
            n = 0
            for op in self.ops[e]:
                if op.sig and op.dma_key is None:
                    n += 1
                    signum[(e, op.idx)] = n
            nsig[e] = n
        esems = {}
        for e in self.ENGS:
            k = (nsig[e] + SEM_CAP - 1) // SEM_CAP
            esems[e] = [stack.enter_context(nc.semaphore("p_%s_%d" % (e, i))) for i in range(max(k, 1))]
        dsems = {}
        for key in self.dma_cnt:
            dsems[key] = stack.enter_context(nc.semaphore("d_" + "_".join(str(x) for x in key)))
        block = stack.enter_context(nc.Block())

        def replay(ename, e):
            waited = {}
            for op in self.ops[ename]:
                need = {}
                for d in op.deps:
                    if d[0] == "e":
                        n = signum[(d[1], d[2])]
                        key = ("e", d[1])
                        val = n
                    else:
                        key = ("d", d[1])
                        val = d[2]
                    if waited.get(key, 0) >= val:
                        continue
                    if need.get(key, 0) < val:
                        need[key] = val
                for key, val in need.items():
                    waited[key] = val
                    if key[0] == "e":
                        si, sv = (val - 1) // SEM_CAP, (val - 1) % SEM_CAP + 1
                        e.wait_ge(esems[key[1]][si], sv)
                    else:
                        e.wait_ge(dsems[key[1]], val)
                if op.fn is None:
                    continue
                ins = op.fn(e)
                if op.dma_key is not None:
                    ins.then_inc(dsems[op.dma_key], 16)
                elif op.sig:
                    n = signum[(ename, op.idx)]
                    ins.then_inc(esems[ename][(n - 1) // SEM_CAP], 1)

        @block.sync
        def _(e):
            replay("sp", e)

        @block.scalar
        def _(e):
            replay("act", e)

        @block.vector
        def _(e):
            replay("dve", e)

        @block.gpsimd
        def _(e):
            replay("pool", e)

        @block.tensor
        def _(e):
            replay("pe", e)


def build_program(debug_stop=None):
    nc = bass.Bass("TRN2", target_bir_lowering=False)
    stack = contextlib.ExitStack()
    P = Prog()

    def din(name, shape, dt=F32):
        return nc.dram_tensor(name, list(shape), dt, kind="ExternalInput").ap()

    x_in = din("x", [S, D])
    smalls_in = din("smalls", [128, NS])
    gbc_in = din("gbc", [128, D])
    dft_in = din("dft", [64, 128, 1024], BF16)
    w_in_in = din("w_in", [D, 1536])
    fmap_in = din("fmap", [4, 128, 128])
    w_out_in = din("w_out", [D, D])
    pmap_in = din("pmap", [4, 256, 256])
    wg_in = din("wg", [2, D, DFF])
    wu_in = din("wu", [2, D, DFF])
    wd_in = din("wd", [2, DFF, D])
    out_d = nc.dram_tensor("out", [S, D], F32, kind="ExternalOutput").ap()

    def sb(name, cols, dt):
        h = stack.enter_context(nc.sbuf_tensor(name, [128, cols], dt))
        return Buf(name, h, dt)

    XTb = sb("XT", NCH * S, F32)
    HTb = sb("HT", NCH * S, BF16)
    RB = 51200
    Rb = sb("R", RB // 2, BF16)
    WBb = sb("WB", 4 * 3072, BF16)
    WDb = sb("WD", 6144, BF16)
    TMPb = sb("TMP", 4096, BF16)
    SMb = sb("SM", NS, F32)
    IDBb = sb("IDB", 128, BF16)
    ONESb = sb("ONES", 128, BF16)
    CSMb = sb("CSM", 1024, BF16)
    FMb = sb("FM", 512, F32)
    EPSb = sb("EPS", 4, F32)
    PSb = []
    for i in range(8):
        h = stack.enter_context(nc.psum_tensor("ps%d" % i, [128, 512], F32))
        PSb.append(Buf("ps%d" % i, h, F32))

    XT = T(XTb, 0, F32)
    HT = T(HTb, 0, BF16)
    SM = T(SMb, 0, F32)
    IDB = T(IDBb, 0, BF16)
    ONES = T(ONESb, 0, BF16)
    CSM = T(CSMb, 0, BF16)
    FM = T(FMb, 0, F32)
    EPS = T(EPSb, 0, F32)
    PS = [T(b, 0, F32) for b in PSb]
    ident = SM(C_ID, 128)

    SQ = [T(TMPb, i * 1024, BF16) for i in range(3)]
    TF = [T(TMPb, 3072, F32), T(TMPb, 5120, F32)]
    TSP = T(TMPb, 7168, F32)

    def mm(out, lhsT, rhs, start, stop):
        return P.add("pe", lambda e: e.matmul(out.ap, lhsT=lhsT.ap, rhs=rhs.ap, start=start, stop=stop),
                     reads=[lhsT, rhs], writes=[out])

    def tr(out, in_):
        return P.add("pe", lambda e: e.transpose(out=out.ap, in_=in_.ap, identity=ident.ap),
                     reads=[in_, ident], writes=[out])

    def copy(eng, out, in_):
        if eng == "act":
            return P.add("act", lambda e: e.activation(out=out.ap, in_=in_.ap, func=AF.Copy), reads=[in_], writes=[out])
        return P.add(eng, lambda e: e.tensor_copy(out=out.ap, in_=in_.ap), reads=[in_], writes=[out])

    def tt_op(eng, out, a, b, op):
        return P.add(eng, lambda e: e.tensor_tensor(out=out.ap, in0=a.ap, in1=b.ap, op=op), reads=[a, b], writes=[out])

    def stt(eng, out, in0, scalar, in1, op0, op1):
        rd = [in0, in1]
        sc = scalar
        if isinstance(scalar, View):
            rd.append(scalar)
            sc = scalar.ap
        return P.add(eng, lambda e: e.scalar_tensor_tensor(out=out.ap, in0=in0.ap, scalar=sc, in1=in1.ap, op0=op0, op1=op1),
                     reads=rd, writes=[out])

    def ts(eng, out, in0, s1, s2, op0, op1=None):
        rd = [in0]
        a1, a2 = s1, s2
        if isinstance(s1, View):
            rd.append(s1)
            a1 = s1.ap
        if isinstance(s2, View):
            rd.append(s2)
            a2 = s2.ap
        if op1 is None:
            return P.add(eng, lambda e: e.tensor_scalar(out=out.ap, in0=in0.ap, scalar1=a1, scalar2=None, op0=op0),
                         reads=rd, writes=[out])
        return P.add(eng, lambda e: e.tensor_scalar(out=out.ap, in0=in0.ap, scalar1=a1, scalar2=a2, op0=op0, op1=op1),
                     reads=rd, writes=[out])

    def act(out, in_, func, scale=1.0, bias=None):
        rd = [in_]
        sc = scale
        if isinstance(scale, View):
            rd.append(scale)
            sc = scale.ap
        if bias is None:
            return P.add("act", lambda e: e.activation(out=out.ap, in_=in_.ap, func=func, scale=sc), reads=rd, writes=[out])
        rd.append(bias)
        return P.add("act", lambda e: e.activation(out=out.ap, in_=in_.ap, func=func, bias=bias.ap, scale=sc),
                     reads=rd, writes=[out])

    def memset(eng, out, val):
        return P.add(eng, lambda e: e.memset(out.ap, val), writes=[out])

    def dma(eng, out_ap, in_ap, key, reads=(), writes=()):
        return P.add(eng, lambda e: e.dma_start(out=out_ap, in_=in_ap), reads=reads, writes=writes, dma_key=key)

    v = SM(0, NS)
    dma("sp", v.ap, smalls_in, ("sm",), writes=[v])
    v = FM(0, 512)
    dma("sp", v.ap.rearrange("p (h f) -> p h f", h=4), fmap_in.rearrange("h e f -> e h f"), ("fm",), writes=[v])
    memset("dve", ONES(0, 128), 1.0)
    memset("dve", EPS(0, 1), RMS_EPS)
    memset("dve", EPS(1, 1), LN_EPS)
    copy("dve", IDB(0, 128), ident)

    STG = [T(Rb, 0, F32), T(Rb, 16384, F32)]
    xv = x_in.rearrange("(g j p) d -> g p j d", j=4, p=128)
    for g in range(4):
        st = STG[g % 2]
        v = st(0, 4096)
        dma("sp", v.ap.rearrange("p (j d) -> p j d", j=4), xv[g], ("stg", g % 2), writes=[v])
        for c in range(NCH):
            b = P.bank()
            for j in range(4):
                tr(PS[b](j * 128, 128), st(j * 1024 + c * 128, 128))
            copy("act" if c % 2 else "dve", XT(c * S + g * TW, TW), PS[b](0, TW))

    def gain(set_i, c):
        return SM(C_GAIN + set_i * 8 + c, 1)

    def rstd_tile(tt, dst):
        b = P.bank()
        for c in range(NCH):
            sq = SQ[c % 3](0, TW)
            xs = XT(c * S + tt * TW, TW)
            act(sq, xs, AF.Square)
            mm(PS[b](0, TW), ONES(0, 128), sq, c == 0, c == NCH - 1)
        std = TF[0](0, TW)
        act(std, PS[b](0, TW), AF.Ln, scale=1.0 / D, bias=EPS(0, 1))
        act(dst, std, AF.Exp, scale=-0.5)

    def norm_tile_to_ht(tt, gset):
        rs = TF[1](0, TW)
        rstd_tile(tt, rs)
        for c in range(NCH):
            stt("dve", HT(c * S + tt * TW, TW), XT(c * S + tt * TW, TW),
                gain(gset, c), rs, ALU.mult, ALU.mult)

    def wblock_dma(dst_view, src2d, col0, ncols, key):
        src = src2d.rearrange("(c p) n -> p c n", p=128)[:, :, col0:col0 + ncols]
        dma("pool", dst_view.ap.rearrange("p (c n) -> p c n", c=NCH), src, key, writes=[dst_view])

    ACTB = T(Rb, 0, BF16)
    WD = T(WDb, 0, BF16)
    WBS = [T(WBb, i * 6144, BF16) for i in range(4)]
    QUARTERS = [(0, 6), (6, 6), (12, 6), (18, 4)]

    def ffn_load_gu(layer, q, hb):
        f0, nf = QUARTERS[q]
        half = nf // 2
        ncols = half * 128
        col0 = (f0 + hb * half) * 128
        wblock_dma(WBS[hb * 2](0, NCH * ncols), wg_in[layer], col0, ncols, ("wb", hb * 2))
        wblock_dma(WBS[hb * 2 + 1](0, NCH * ncols), wu_in[layer], col0, ncols, ("wb", hb * 2 + 1))

    def ffn_load_wd(layer, q):
        f0, nf = QUARTERS[q]
        v = WD(0, nf * 1024)
        src = wd_in[layer][f0 * 128:(f0 + nf) * 128, :].rearrange("(f p) n -> p f n", p=128)
        dma("pool", v.ap.rearrange("p (f n) -> p f n", f=nf), src, ("wd",), writes=[v])

    def ffn(layer, pre_last_q=None, hook=None):
        for q, (f0, nf) in enumerate(QUARTERS):
            half = nf // 2
            ncols = half * 128
            if q == 3 and pre_last_q is not None:
                pre_last_q()
            for hb in range(2):
                wgv, wuv = WBS[hb * 2], WBS[hb * 2 + 1]
                for fl in range(half):
                    fq = hb * half + fl
                    for tt in range(NTT):
                        bg, bu = P.bank(), P.bank()
                        for c in range(NCH):
                            mm(PS[bg](0, TW), wgv(c * ncols + fl * 128, 128), HT(c * S + tt * TW, TW), c == 0, c == NCH - 1)
                        for c in range(NCH):
                            mm(PS[bu](0, TW), wuv(c * ncols + fl * 128, 128), HT(c * S + tt * TW, TW), c == 0, c == NCH - 1)
                        sg = SQ[(fq * NTT + tt) % 3](0, TW)
                        act(sg, PS[bg](0, TW), AF.Silu)
                        tt_op("dve", ACTB(fq * S + tt * TW, TW), PS[bu](0, TW), sg, ALU.mult)
                if q + 1 < 4:
                    ffn_load_gu(layer, q + 1, hb)
            for tt in range(NTT):
                for dc in range(NCH):
                    b = P.bank()
                    for fl in range(nf):
                        mm(PS[b](0, TW), WD(fl * 1024 + dc * 128, 128), ACTB(fl * S + tt * TW, TW), fl == 0, fl == nf - 1)
                    xs = XT(dc * S + tt * TW, TW)
                    tt_op("dve", xs, xs, PS[b](0, TW), ALU.add)
                if q == 3 and hook is not None:
                    hook(tt)
            if q + 1 < 4:
                ffn_load_wd(layer, q + 1)

    for tt in range(NTT):
        norm_tile_to_ht(tt, 0)
    chunk_cols = [0, 128, 256, 384]
    for j in range(4):
        chunk_cols += [512 + 128 * j, 1024 + 128 * j]
    WSL = [T(WBb, i * 2048, BF16) for i in range(12)]
    for k, col0 in enumerate(chunk_cols):
        wblock_dma(WSL[k](0, 1024), w_in_in, col0, 128, ("wsl", k))
    A = T(Rb, 0, BF16)
    U = T(Rb, 16384, BF16)
    UW = S + 30
    YB = T(Rb, 33024, BF16)
    for j in range(4):
        memset("pool", U(j * UW, 15), 0.0)
        memset("pool", U(j * UW + 15 + S, 15), 0.0)
    n_ev = 0
    for j in range(4):
        for tt in range(NTT):
            b = P.bank()
            for c in range(NCH):
                mm(PS[b](0, TW), WSL[j](c * 128, 128), HT(c * S + tt * TW, TW), c == 0, c == NCH - 1)
            copy("act" if n_ev % 2 else "dve", A(j * S + tt * TW, TW), PS[b](0, TW))
            n_ev += 1
    for j in range(4):
        for tt in range(NTT):
            bv, bg = P.bank(), P.bank()
            for c in range(NCH):
                mm(PS[bv](0, TW), WSL[4 + 2 * j](c * 128, 128), HT(c * S + tt * TW, TW), c == 0, c == NCH - 1)
            for c in range(NCH):
                mm(PS[bg](0, TW), WSL[5 + 2 * j](c * 128, 128), HT(c * S + tt * TW, TW), c == 0, c == NCH - 1)
            sig = TF[(j * NTT + tt) % 2](0, TW)
            act(sig, PS[bg](0, TW), AF.Sigmoid)
            tt_op("dve", U(j * UW + 15 + tt * TW, TW), PS[bv](0, TW), sig, ALU.mult)

    WOB = [(0, 384), (384, 384), (768, 256)]
    for k, (c0, ncl) in enumerate(WOB):
        wblock_dma(WBS[k](0, NCH * ncl), w_out_in, c0, ncl, ("wb", k))

    WDS = [T(WDb, i * 2048, BF16) for i in range(6)]
    dft_state = {"next": 0}

    def dft_prefetch(upto):
        while dft_state["next"] < min(upto, 64):
            i = dft_state["next"]
            v = WDS[i % 6](0, 1024)
            dma("sp", v.ap, dft_in[i], ("wds", i % 6), writes=[v])
            dft_state["next"] += 1

    dft_prefetch(6)

    DS = [T(HTb, 0, BF16), T(HTb, 7936, BF16)]
    CT0 = 15872

    def cset(k):
        base = CT0 + k * 5120
        return dict(cb=T(HTb, base, F32), cbb=T(HTb, base + 2048, BF16), rs=T(HTb, base + 3072, F32))

    CS = [cset(k) for k in range(3)]
    tiles = [(j, tt) for j in range(4) for tt in range(NTT)]
    pbank = {}

    def build_D(j):
        for t in range(31):
            dst = DS[j % 2](t * 128, 128)
            sc = SM(C_CW + j * 31 + t, 1)
            ts("dve", dst, IDB(0, 128), sc, None, ALU.mult)

    def conv_s0(i):
        j, tt = tiles[i]
        if i == 0:
            build_D(0)
        if tt == 0 and j + 1 < 4:
            build_D(j + 1)
        b = P.bank()
        for t in range(31):
            mm(PS[b](0, TW), DS[j % 2](t * 128, 128), U(j * UW + tt * TW + t, TW), t == 0, t == 30)
        cs = CS[i % 3]
        ts("dve", cs["cb"](0, TW), PS[b](0, TW), SM(C_CB + j, 1), None, ALU.add)
        copy("dve", cs["cbb"](0, TW), cs["cb"](0, TW))

    def conv_s2(i):
        cs = CS[i % 3]
        b = P.bank()
        mm(PS[b](0, TW), ONES(0, 128), cs["cbb"](0, TW), True, True)
        stt("dve", cs["cb"](0, TW), PS[b](0, TW), -1.0 / 128, cs["cb"](0, TW), ALU.mult, ALU.add)
        tt_op("dve", cs["cbb"](0, TW), cs["cb"](0, TW), cs["cb"](0, TW), ALU.mult)

    def conv_s3(i):
        j, tt = tiles[i]
        cs = CS[i % 3]
        b = P.bank()
        mm(PS[b](0, TW), ONES(0, 128), cs["cbb"](0, TW), True, True)
        act(cs["rs"](0, TW), PS[b](0, TW), AF.Ln, scale=1.0 / 128, bias=EPS(1, 1))
        act(cs["rs"](0, TW), cs["rs"](0, TW), AF.Exp, scale=-0.5)
        tt_op("dve", cs["cb"](0, TW), cs["cb"](0, TW), cs["rs"](0, TW), ALU.mult)
        act(YB(j * S + tt * TW, TW), cs["cb"](0, TW), AF.Silu, scale=SM(C_LG + j, 1), bias=SM(C_LB + j, 1))

    for i in range(16 + 2):
        if i < 16:
            conv_s0(i)
        if 0 <= i - 1 < 16:
            conv_s2(i - 1)
        if 0 <= i - 2 < 16:
            conv_s3(i - 2)

    for h in range(4):
        b = P.bank()
        mm(PS[b](0, 128), SM(C_CD, 128), FM(h * 128, 128), True, True)
        mm(PS[b](128, 128), SM(C_SD, 128), FM(h * 128, 128), True, True)
        copy("dve", CSM(h * 256, 256), PS[b](0, 256))
    Y = T(HTb, 0, BF16)
    n_ev = 0
    for nt in range(16):
        for hp in range(2):
            b = P.bank()
            for hh in range(2):
                h = 2 * hp + hh
                mm(PS[b](hh * 256, 256), A(h * S + nt * 128, 128), CSM(h * 256, 256), True, True)
            copy("act" if n_ev % 2 else "dve", Y(nt * 1024 + hp * 512, 512), PS[b](0, 512))
            n_ev += 1
    YA = A
    for kt in range(4):
        banks = [P.bank() for _ in range(4)]
        for nt in range(16):
            i = kt * 16 + nt
            dft_prefetch(i + 6)
            tab = WDS[i % 6]
            for h in range(4):
                mm(PS[banks[h]](0, TW), Y(nt * 1024 + h * 256, 128), tab(0, 512), nt == 0, False)
                mm(PS[banks[h]](0, TW), Y(nt * 1024 + h * 256 + 128, 128), tab(512, 512), False, nt == 15)
        for h in range(4):
            copy("act" if h % 2 else "dve", YA(h * S + kt * TW, TW), PS[banks[h]](0, TW))

    ffn_load_wd(0, 0)

    for dc in range(NCH):
        blk = (dc * 128) // 384
        off = (dc * 128) % 384
        ncl = WOB[blk][1]
        for tt in range(NTT):
            b = P.bank()
            for kc in range(NCH):
                rhs = YA(kc * S + tt * TW, TW) if kc < 4 else YB((kc - 4) * S + tt * TW, TW)
                mm(PS[b](0, TW), WBS[blk](kc * ncl + off, 128), rhs, kc == 0, kc == NCH - 1)
            xs = XT(dc * S + tt * TW, TW)
            tt_op("dve", xs, xs, PS[b](0, TW), ALU.add)

    if debug_stop == "mix0":
        return finish_debug(nc, stack, P, XT, out_d)

    ffn_load_gu(0, 0, 0)
    ffn_load_gu(0, 0, 1)
    for tt in range(NTT):
        norm_tile_to_ht(tt, 1)

    RSTD = T(Rb, 24704, F32)
    PM = T(WDb, 8192, BF16)

    def load_pm():
        v = PM(0, 2048)
        dma("pool", v.ap.rearrange("p (a n) -> p a n", a=8), pmap_in.rearrange("g (kc p) n -> p (g kc) n", p=128),
            ("pm",), writes=[v])

    def pool_norm_hook(tt):
        rstd_tile(tt, RSTD(tt * TW, TW))

    ffn(0, pre_last_q=load_pm, hook=pool_norm_hook)

    if debug_stop == "ffn0":
        return finish_debug(nc, stack, P, XT, out_d)

    ffn_load_gu(1, 0, 0)
    ffn_load_gu(1, 0, 1)
    HBW = S + 16
    PG = T(Rb, 0, BF16)
    HB = [T(Rb, 8192, BF16), T(Rb, 8192 + 4160, BF16)]
    ET = TSP
    for k in range(2):
        memset("pool", HB[k](0, 8), 0.0)
        memset("pool", HB[k](8 + S, 8), 0.0)
    def compute_h(c):
        Hc = HB[c % 2]
        for tt in range(NTT):
            stt("dve", Hc(8 + tt * TW, TW), XT(c * S + tt * TW, TW), gain(2, c), RSTD(tt * TW, TW), ALU.mult, ALU.mult)

    for c in range(NCH):
        gi = c // 2
        cc = c % 2
        w = 2 << gi
        half = w // 2
        H = HB[c % 2]
        if c == 0:
            compute_h(0)
        pbanks = []
        for tt in range(NTT):
            b = P.bank()
            pbanks.append(b)
            for t in range(w):
                mm(PS[b](0, TW), IDB(0, 128), H(8 + tt * TW + t - half, TW), t == 0, t == w - 1)
        if c + 1 < NCH and (c + 1) % 2 == 1:
            compute_h(c + 1)
        for tt in range(NTT):
            b = pbanks[tt]
            stt("dve", PG(cc * S + tt * TW, TW), PS[b](0, TW), 1.0 / w, H(8 + tt * TW, TW), ALU.mult, ALU.subtract)
            if tt == 0:
                tt_op("dve", ET(0, 8), PS[b](0, 8), SM(C_IC + gi * 16, 8), ALU.mult)
                tt_op("pool", PG(cc * S, 8), ET(0, 8), H(8, 8), ALU.subtract)
            if tt == NTT - 1:
                tt_op("dve", ET(8, 8), PS[b](TW - 8, 8), SM(C_IC + gi * 16 + 8, 8), ALU.mult)
                tt_op("pool", PG(cc * S + S - 8, 8), ET(8, 8), H(8 + S - 8, 8), ALU.subtract)
        if cc == 1:
            for dcl in range(2):
                co = gi * 2 + dcl
                for tt in range(NTT):
                    b = P.bank()
                    for kc in range(2):
                        mm(PS[b](0, TW), PM((gi * 2 + kc) * 256 + dcl * 128, 128), PG(kc * S + tt * TW, TW), kc == 0, kc == 1)
                    xs = XT(co * S + tt * TW, TW)
                    stt("dve", xs, PS[b](0, TW), SM(C_PS + co, 1), xs, ALU.mult, ALU.add)
                    if dcl == 0 and tt == 0 and c + 1 < NCH:
                        compute_h(c + 1)

    if debug_stop == "mix1":
        return finish_debug(nc, stack, P, XT, out_d)

    ffn_load_wd(1, 0)
    for tt in range(NTT):
        norm_tile_to_ht(tt, 3)

    GB = T(Rb, 24576, F32)
    OUTT = [T(Rb, 36864, F32), T(Rb, 40960, F32)]
    ov = out_d.rearrange("(t p) d -> t p d", p=128)
    out_toks = []

    def load_gb():
        v = GB(0, D)
        dma("sp", v.ap, gbc_in, ("gb",), writes=[v])

    def final_hook(tt):
        bss = [P.bank() for _ in range(4)]
        for c in range(NCH):
            sq = SQ[c % 3](0, TW)
            act(sq, XT(c * S + tt * TW, TW), AF.Square)
            for t4 in range(4):
                mm(PS[bss[t4]](0, 1), SQ[c % 3](t4 * 128, 128), ONES(0, 1), c == 0, c == NCH - 1)
        k2 = tt % 2
        l1 = TSP(k2 * 16, 4)
        l2 = TSP(k2 * 16 + 4, 4)
        r4 = TSP(k2 * 16 + 8, 4)
        for t4 in range(4):
            act(TSP(k2 * 16 + t4, 1), PS[bss[t4]](0, 1), AF.Ln, scale=1.0 / D, bias=EPS(0, 1))
        ts("dve", l2, l1, -0.5, None, ALU.mult)
        act(r4, l2, AF.Exp)
        for t4 in range(4):
            ti = tt * 4 + t4
            k = ti % 2
            for hf in range(2):
                b = P.bank()
                for cc in range(4):
                    c = hf * 4 + cc
                    tr(PS[b](cc * 128, 128), XT(c * S + ti * 128, 128))
                stt("dve", OUTT[k](hf * TW, TW), PS[b](0, TW), TSP(k2 * 16 + 8 + t4, 1), GB(hf * TW, TW), ALU.mult, ALU.mult)
            o = OUTT[k](0, D)
            out_toks.append(dma("sp", ov[ti], o.ap, ("out", k), reads=[o]))

    if debug_stop == "ffn1":
        ffn(1)
        return finish_debug(nc, stack, P, XT, out_d)
    ffn(1, pre_last_q=load_gb, hook=final_hook)
    P.add("sp", None, extra_deps=out_toks)
    P.emit(nc, stack)
    stack.close()
    return nc


def finish_debug(nc, stack, P, XT, out_d):
    toks = []
    ov = out_d.rearrange("(c p) (a n) -> c p (a n)", p=128, a=1)
    ov2 = out_d.rearrange("(c h p) n -> c h p n", c=8, h=2)
    for c in range(NCH):
        for h in range(2):
            v = XT(c * S + h * 1024, 1024)
            toks.append(P.add("sp", (lambda e, v=v, c=c, h=h: e.dma_start(out=ov2[c, h], in_=v.ap)), reads=[v],
                              dma_key=("out", (c * 2 + h) % 2)))
    P.add("sp", None, extra_deps=toks)
    P.emit(nc, stack)
    stack.close()
    return nc


def _consts():
    n = np.arange(S, dtype=np.int64)
    ang = 2.0 * np.pi * ((n[:, None] * n[None, :]) % S).astype(np.float64) / S
    Cs = np.cos(ang)
    Sn = -np.sin(ang)
    tab = np.empty((4, 16, 128, 2, 512), dtype=np.float32)
    for kt in range(4):
        for nt in range(16):
            tab[kt, nt, :, 0, :] = Cs[nt * 128:(nt + 1) * 128, kt * 512:(kt + 1) * 512]
            tab[kt, nt, :, 1, :] = Sn[nt * 128:(nt + 1) * 128, kt * 512:(kt + 1) * 512]
    tab = tab.reshape(64, 128, 1024).astype(ml_dtypes.bfloat16)
    e = np.arange(128, dtype=np.int64)
    ang = 2.0 * np.pi * ((e[:, None] * e[None, :]) % 128).astype(np.float64) / 128
    Cd = (np.cos(ang) / 512.0).astype(np.float32)
    Sd = (np.sin(ang) / 512.0).astype(np.float32)
    ic = np.zeros((4, 16), dtype=np.float32)
    for gi in range(4):
        w = 2 << gi
        for i in range(16):
            pos = i if i < 8 else S - 16 + i
            lo = min(max(pos - w // 2, 0), S)
            hi = min(max(pos - w // 2 + w, 0), S)
            ic[gi, i] = 1.0 / float(hi - lo)
    return tab, Cd, Sd, ic


def _prep(inputs):
    tab, Cd, Sd, ic = _consts()
    f = lambda a: np.ascontiguousarray(np.asarray(a, dtype=np.float32))
    sm = np.zeros((128, NS), dtype=np.float32)
    sm[:, C_ID:C_ID + 128] = np.eye(128, dtype=np.float32)
    sm[:, C_CD:C_CD + 128] = Cd
    sm[:, C_SD:C_SD + 128] = Sd
    nm, nf_ = f(inputs["norm_mix_g"]), f(inputs["norm_ffn_g"])
    for si, g in enumerate([nm[0], nf_[0], nm[1], nf_[1]]):
        sm[:, C_GAIN + si * 8:C_GAIN + si * 8 + 8] = g.reshape(8, 128).T
    cw = f(inputs["conv_w"])[0]
    sm[:, C_CW:C_CW + 124] = cw.reshape(31, 4, 128).transpose(2, 1, 0).reshape(128, 124)
    sm[:, C_CB:C_CB + 4] = f(inputs["conv_b"])[0].reshape(4, 128).T
    sm[:, C_LG:C_LG + 4] = f(inputs["conv_ln_g"])[0].reshape(4, 128).T
    sm[:, C_LB:C_LB + 4] = f(inputs["conv_ln_b"])[0].reshape(4, 128).T
    sm[:, C_PS:C_PS + 8] = f(inputs["pool_scale"])[0].reshape(8, 128).T
    sm[:, C_IC:C_IC + 64] = np.broadcast_to(ic.reshape(1, 64), (128, 64))
    gbc = np.ascontiguousarray(np.broadcast_to(f(inputs["final_g"]).reshape(1, D), (128, D)))
    shared = dict(
        smalls=sm, gbc=gbc, dft=tab,
        w_in=f(inputs["w_in_ab"])[0], fmap=f(inputs["fnet_map"])[0], w_out=f(inputs["w_out_ab"])[0],
        pmap=f(inputs["pool_map"])[0], wg=f(inputs["ffn_w_gate"]), wu=f(inputs["ffn_w_up"]), wd=f(inputs["ffn_w_down"]),
    )
    x = f(inputs["x"])
    return [dict(shared, x=x[b]) for b in range(8)]


_NC_CACHE = {}


def kernel(**inputs):
    in_maps = _prep(inputs)
    if "nc" not in _NC_CACHE:
        _NC_CACHE["nc"] = build_program()
    nc = _NC_CACHE["nc"]
    res = run_bass_kernel_spmd(nc, in_maps, core_ids=list(range(8)))
    return np.stack([np.asarray(r["out"], dtype=np.float32) for r in res.results], axis=0)
```

```python
import contextlib
import numpy as np
import ml_dtypes
import concourse.bass as bass
import concourse.mybir as mybir
from concourse.bass_utils import run_bass_kernel_spmd

F32 = mybir.dt.float32
BF16 = mybir.dt.bfloat16
AF = mybir.ActivationFunctionType
ALU = mybir.AluOpType
AX = mybir.AxisListType

S = 2048
D = 1024
NCH = 8
TW = 512
NTT = 4
DFF = 2816
NF = 22
RMS_EPS = 1e-6
LN_EPS = 1e-5
ESZ = {F32: 4, BF16: 2}

C_ID = 0
C_CD = 128
C_SD = 256
C_GAIN = 384
C_CW = 416
C_CB = 540
C_LG = 544
C_LB = 548
C_PS = 552
C_IC = 560
NS = 624

GRAN = 256
SEM_CAP = 3000


class Buf:
    def __init__(self, name, handle, base_dt):
        self.name = name
        self.h = handle
        self.base_dt = base_dt
        self.besz = ESZ[base_dt]
        self.state = {}


class View:
    __slots__ = ("ap", "buf", "lo", "hi")

    def __init__(self, ap, buf, lo, hi):
        self.ap, self.buf, self.lo, self.hi = ap, buf, lo, hi


class T:
    def __init__(self, buf, off, dt):
        self.buf, self.off, self.dt, self.esz = buf, off, dt, ESZ[dt]

    def __call__(self, lo, n):
        b0 = self.off + lo * self.esz
        b1 = b0 + n * self.esz
        bes = self.buf.besz
        assert b0 % bes == 0 and b1 % bes == 0, (self.buf.name, b0, b1)
        ap = self.buf.h[:, b0 // bes:b1 // bes]
        if self.dt != self.buf.base_dt:
            assert b0 % 4 == 0 and b1 % 4 == 0
            ap = ap.bitcast(self.dt)
        return View(ap, self.buf, b0, b1)


class Op:
    __slots__ = ("eng", "fn", "deps", "dma_key", "dma_val", "idx", "sig")


class Prog:
    ENGS = ("pe", "act", "dve", "pool", "sp")

    def __init__(self):
        self.ops = {e: [] for e in self.ENGS}
        self.dma_cnt = {}
        self.nbank = 0

    def _tok(self, op):
        if op.dma_key is not None:
            return ("d", op.dma_key, op.dma_val)
        return ("e", op.eng, op.idx)

    def add(self, eng, fn, reads=(), writes=(), dma_key=None, extra_deps=()):
        op = Op()
        op.eng, op.fn, op.dma_key, op.sig = eng, fn, dma_key, False
        op.idx = len(self.ops[eng])
        op.dma_val = None
        if dma_key is not None:
            self.dma_cnt[dma_key] = self.dma_cnt.get(dma_key, 0) + 16
            op.dma_val = self.dma_cnt[dma_key]
        tok = self._tok(op)
        deps = set(extra_deps)
        for v in reads:
            st = v.buf.state
            for g in range(v.lo // GRAN, (v.hi - 1) // GRAN + 1):
                ent = st.get(g)
                if ent is None:
                    ent = st[g] = [None, []]
                if ent[0] is not None:
                    deps.add(ent[0])
                ent[1].append(tok)
        for v in writes:
            st = v.buf.state
            for g in range(v.lo // GRAN, (v.hi - 1) // GRAN + 1):
                ent = st.get(g)
                if ent is None:
                    ent = st[g] = [None, []]
                if ent[0] is not None:
                    deps.add(ent[0])
                for r in ent[1]:
                    deps.add(r)
                ent[0] = tok
                ent[1] = []
        deps.discard(tok)
        op.deps = [d for d in deps if not (d[0] == "e" and d[1] == eng)]
        self.ops[eng].append(op)
        return tok

    def bank(self):
        b = self.nbank % 8
        self.nbank += 1
        return b

    def emit(self, nc, stack):
        for e in self.ENGS:
            for op in self.ops[e]:
                for d in op.deps:
                    if d[0] == "e":
                        self.ops[d[1]][d[2]].sig = True
        signum = {}
        nsig = {}
        for e in self.ENGS:
            n = 0
            for op in self.ops[e]:
                if op.sig and op.dma_key is None:
                    n += 1
                    signum[(e, op.idx)] = n
            nsig[e] = n
        esems = {}
        for e in self.ENGS:
            k = (nsig[e] + SEM_CAP - 1) // SEM_CAP
            esems[e] = [stack.enter_context(nc.semaphore("p_%s_%d" % (e, i))) for i in range(max(k, 1))]
        dsems = {}
        for key in self.dma_cnt:
            dsems[key] = stack.enter_context(nc.semaphore("d_" + "_".join(str(x) for x in key)))
        block = stack.enter_context(nc.Block())

        def replay(ename, e):
            waited = {}
            for op in self.ops[ename]:
                need = {}
                for d in op.deps:
                    if d[0] == "e":
                        n = signum[(d[1], d[2])]
                        key = ("e", d[1])
                        val = n
                    else:
                        key = ("d", d[1])
                        val = d[2]
                    if waited.get(key, 0) >= val:
                        continue
                    if need.get(key, 0) < val:
                        need[key] = val
                for key, val in need.items():
                    waited[key] = val
                    if key[0] == "e":
                        si, sv = (val - 1) // SEM_CAP, (val - 1) % SEM_CAP + 1
                        e.wait_ge(esems[key[1]][si], sv)
                    else:
                        e.wait_ge(dsems[key[1]], val)
                if op.fn is None:
                    continue
                ins = op.fn(e)
                if op.dma_key is not None:
                    ins.then_inc(dsems[op.dma_key], 16)
                elif op.sig:
                    n = signum[(ename, op.idx)]
                    ins.then_inc(esems[ename][(n - 1) // SEM_CAP], 1)

        @block.sync
        def _(e):
            replay("sp", e)

        @block.scalar
        def _(e):
            replay("act", e)

        @block.vector
        def _(e):
            replay("dve", e)

        @block.gpsimd
        def _(e):
            replay("pool", e)

        @block.tensor
        def _(e):
            replay("pe", e)


def build_program(debug_stop=None):
    nc = bass.Bass("TRN2", target_bir_lowering=False)
    stack = contextlib.ExitStack()
    P = Prog()

    def din(name, shape, dt=F32):
        return nc.dram_tensor(name, list(shape), dt, kind="ExternalInput").ap()

    x_in = din("x", [S, D])
    smalls_in = din("smalls", [128, NS])
    gbc_in = din("gbc", [128, D])
    dft_in = din("dft", [64, 128, 1024], BF16)
    w_in_in = din("w_in", [D, 1536])
    fmap_in = din("fmap", [4, 128, 128])
    w_out_in = din("w_out", [D, D])
    pmap_in = din("pmap", [4, 256, 256])
    wg_in = din("wg", [2, D, DFF])
    wu_in = din("wu", [2, D, DFF])
    wd_in = din("wd", [2, DFF, D])
    out_d = nc.dram_tensor("out", [S, D], F32, kind="ExternalOutput").ap()

    def sb(name, cols, dt):
        h = stack.enter_context(nc.sbuf_tensor(name, [128, cols], dt))
        return Buf(name, h, dt)

    XTb = sb("XT", NCH * S, F32)
    HTb = sb("HT", NCH * S, BF16)
    RB = 51200
    Rb = sb("R", RB // 2, BF16)
    WBb = sb("WB", 4 * 3072, BF16)
    WDb = sb("WD", 6144, BF16)
    TMPb = sb("TMP", 4096, BF16)
    SMb = sb("SM", NS, F32)
    IDBb = sb("IDB", 128, BF16)
    ONESb = sb("ONES", 128, BF16)
    CSMb = sb("CSM", 1024, BF16)
    FMb = sb("FM", 512, F32)
    EPSb = sb("EPS", 4, F32)
    CXb = sb("CX", 2048, BF16)
    PSb = []
    for i in range(8):
        h = stack.enter_context(nc.psum_tensor("ps%d" % i, [128, 512], F32))
        PSb.append(Buf("ps%d" % i, h, F32))

    XT = T(XTb, 0, F32)
    HT = T(HTb, 0, BF16)
    SM = T(SMb, 0, F32)
    IDB = T(IDBb, 0, BF16)
    ONES = T(ONESb, 0, BF16)
    CSM = T(CSMb, 0, BF16)
    FM = T(FMb, 0, F32)
    EPS = T(EPSb, 0, F32)
    CX = T(CXb, 0, BF16)
    PS = [T(b, 0, F32) for b in PSb]
    ident = SM(C_ID, 128)

    SQ = [T(TMPb, i * 1024, BF16) for i in range(3)]
    TF = [T(TMPb, 3072, F32), T(TMPb, 5120, F32)]
    TSP = T(TMPb, 7168, F32)

    def mm(out, lhsT, rhs, start, stop):
        return P.add("pe", lambda e: e.matmul(out.ap, lhsT=lhsT.ap, rhs=rhs.ap, start=start, stop=stop),
                     reads=[lhsT, rhs], writes=[out])

    def tr(out, in_):
        return P.add("pe", lambda e: e.transpose(out=out.ap, in_=in_.ap, identity=ident.ap),
                     reads=[in_, ident], writes=[out])

    def copy(eng, out, in_):
        if eng == "act":
            return P.add("act", lambda e: e.activation(out=out.ap, in_=in_.ap, func=AF.Copy), reads=[in_], writes=[out])
        return P.add(eng, lambda e: e.tensor_copy(out=out.ap, in_=in_.ap), reads=[in_], writes=[out])

    def tt_op(eng, out, a, b, op):
        return P.add(eng, lambda e: e.tensor_tensor(out=out.ap, in0=a.ap, in1=b.ap, op=op), reads=[a, b], writes=[out])

    def stt(eng, out, in0, scalar, in1, op0, op1):
        rd = [in0, in1]
        sc = scalar
        if isinstance(scalar, View):
            rd.append(scalar)
            sc = scalar.ap
        return P.add(eng, lambda e: e.scalar_tensor_tensor(out=out.ap, in0=in0.ap, scalar=sc, in1=in1.ap, op0=op0, op1=op1),
                     reads=rd, writes=[out])

    def ts(eng, out, in0, s1, s2, op0, op1=None):
        rd = [in0]
        a1, a2 = s1, s2
        if isinstance(s1, View):
            rd.append(s1)
            a1 = s1.ap
        if isinstance(s2, View):
            rd.append(s2)
            a2 = s2.ap
        if op1 is None:
            return P.add(eng, lambda e: e.tensor_scalar(out=out.ap, in0=in0.ap, scalar1=a1, scalar2=None, op0=op0),
                         reads=rd, writes=[out])
        return P.add(eng, lambda e: e.tensor_scalar(out=out.ap, in0=in0.ap, scalar1=a1, scalar2=a2, op0=op0, op1=op1),
                     reads=rd, writes=[out])

    def act(out, in_, func, scale=1.0, bias=None):
        rd = [in_]
        sc = scale
        if isinstance(scale, View):
            rd.append(scale)
            sc = scale.ap
        if bias is None:
            return P.add("act", lambda e: e.activation(out=out.ap, in_=in_.ap, func=func, scale=sc), reads=rd, writes=[out])
        rd.append(bias)
        return P.add("act", lambda e: e.activation(out=out.ap, in_=in_.ap, func=func, bias=bias.ap, scale=sc),
                     reads=rd, writes=[out])

    def memset(eng, out, val):
        return P.add(eng, lambda e: e.memset(out.ap, val), writes=[out])

    def dma(eng, out_ap, in_ap, key, reads=(), writes=()):
        return P.add(eng, lambda e: e.dma_start(out=out_ap, in_=in_ap), reads=reads, writes=writes, dma_key=key)

    v = SM(0, NS)
    dma("sp", v.ap, smalls_in, ("sm",), writes=[v])
    v = FM(0, 512)
    dma("sp", v.ap.rearrange("p (h f) -> p h f", h=4), fmap_in.rearrange("h e f -> e h f"), ("fm",), writes=[v])
    memset("dve", ONES(0, 128), 1.0)
    memset("dve", EPS(0, 1), RMS_EPS)
    memset("dve", EPS(1, 1), LN_EPS)
    copy("dve", IDB(0, 128), ident)

    STG = [T(Rb, 0, F32), T(Rb, 16384, F32)]
    xv = x_in.rearrange("(g j p) d -> g p j d", j=4, p=128)

    def load_group(g):
        st = STG[g % 2]
        v = st(0, 4096)
        dma("sp", v.ap.rearrange("p (j d) -> p j d", j=4), xv[g], ("stg", g % 2), writes=[v])

    def transpose_group(g):
        st = STG[g % 2]
        for c in range(NCH):
            b = P.bank()
            for j in range(4):
                tr(PS[b](j * 128, 128), st(j * 1024 + c * 128, 128))
            copy("act" if c % 2 else "dve", XT(c * S + g * TW, TW), PS[b](0, TW))

    def gain(set_i, c):
        return SM(C_GAIN + set_i * 8 + c, 1)

    def rstd_tile(tt, dst):
        b = P.bank()
        for c in range(NCH):
            sq = SQ[c % 3](0, TW)
            xs = XT(c * S + tt * TW, TW)
            act(sq, xs, AF.Square)
            mm(PS[b](0, TW), ONES(0, 128), sq, c == 0, c == NCH - 1)
        std = TF[0](0, TW)
        act(std, PS[b](0, TW), AF.Ln, scale=1.0 / D, bias=EPS(0, 1))
        act(dst, std, AF.Exp, scale=-0.5)

    def norm_tile_to_ht(tt, gset):
        rs = TF[1](0, TW)
        rstd_tile(tt, rs)
        for c in range(NCH):
            stt("dve", HT(c * S + tt * TW, TW), XT(c * S + tt * TW, TW),
                gain(gset, c), rs, ALU.mult, ALU.mult)

    def wblock_dma(dst_view, src2d, col0, ncols, key):
        src = src2d.rearrange("(c p) n -> p c n", p=128)[:, :, col0:col0 + ncols]
        dma("pool", dst_view.ap.rearrange("p (c n) -> p c n", c=NCH), src, key, writes=[dst_view])

    ACTB = T(Rb, 0, BF16)
    WD = T(WDb, 0, BF16)
    WBS = [T(WBb, i * 6144, BF16) for i in range(4)]
    QUARTERS = [(0, 6), (6, 6), (12, 6), (18, 4)]

    def ffn_load_gu(layer, q, hb):
        f0, nf = QUARTERS[q]
        half = nf // 2
        ncols = half * 128
        col0 = (f0 + hb * half) * 128
        wblock_dma(WBS[hb * 2](0, NCH * ncols), wg_in[layer], col0, ncols, ("wb", hb * 2))
        wblock_dma(WBS[hb * 2 + 1](0, NCH * ncols), wu_in[layer], col0, ncols, ("wb", hb * 2 + 1))

    def ffn_load_wd(layer, q):
        f0, nf = QUARTERS[q]
        v = WD(0, nf * 1024)
        src = wd_in[layer][f0 * 128:(f0 + nf) * 128, :].rearrange("(f p) n -> p f n", p=128)
        dma("pool", v.ap.rearrange("p (f n) -> p f n", f=nf), src, ("wd",), writes=[v])

    def ffn(layer, pre_last_q=None, hook=None):
        for q, (f0, nf) in enumerate(QUARTERS):
            half = nf // 2
            ncols = half * 128
            if q == 3 and pre_last_q is not None:
                pre_last_q()
            for hb in range(2):
                wgv, wuv = WBS[hb * 2], WBS[hb * 2 + 1]
                for fl in range(half):
                    fq = hb * half + fl
                    for tt in range(NTT):
                        bg, bu = P.bank(), P.bank()
                        for c in range(NCH):
                            mm(PS[bg](0, TW), wgv(c * ncols + fl * 128, 128), HT(c * S + tt * TW, TW), c == 0, c == NCH - 1)
                        for c in range(NCH):
                            mm(PS[bu](0, TW), wuv(c * ncols + fl * 128, 128), HT(c * S + tt * TW, TW), c == 0, c == NCH - 1)
                        sg = SQ[(fq * NTT + tt) % 3](0, TW)
                        act(sg, PS[bg](0, TW), AF.Silu)
                        tt_op("dve", ACTB(fq * S + tt * TW, TW), PS[bu](0, TW), sg, ALU.mult)
                if q + 1 < 4:
                    ffn_load_gu(layer, q + 1, hb)
            for tt in range(NTT):
                for dc in range(NCH):
                    b = P.bank()
                    for fl in range(nf):
                        mm(PS[b](0, TW), WD(fl * 1024 + dc * 128, 128), ACTB(fl * S + tt * TW, TW), fl == 0, fl == nf - 1)
                    xs = XT(dc * S + tt * TW, TW)
                    tt_op("dve", xs, xs, PS[b](0, TW), ALU.add)
                if q == 3 and hook is not None:
                    hook(tt)
            if q + 1 < 4:
                ffn_load_wd(layer, q + 1)

    chunk_cols = [0, 128, 256, 384]
    for j in range(4):
        chunk_cols += [512 + 128 * j, 1024 + 128 * j]
    WSL = [T(WBb, i * 2048, BF16) for i in range(12)]
    for k, col0 in enumerate(chunk_cols):
        wblock_dma(WSL[k](0, 1024), w_in_in, col0, 128, ("wsl", k))
    A = T(Rb, 0, BF16)
    U = T(Rb, 16384, BF16)
    UW = S + 30
    YB = T(Rb, 33024, BF16)
    ev_cnt = {"n": 0}

    def w_in_tile(tt):
        for j in range(4):
            b = P.bank()
            for c in range(NCH):
                mm(PS[b](0, TW), WSL[j](c * 128, 128), HT(c * S + tt * TW, TW), c == 0, c == NCH - 1)
            copy("act" if ev_cnt["n"] % 2 else "dve", A(j * S + tt * TW, TW), PS[b](0, TW))
            ev_cnt["n"] += 1
        for j in range(4):
            bv, bg = P.bank(), P.bank()
            for c in range(NCH):
                mm(PS[bv](0, TW), WSL[4 + 2 * j](c * 128, 128), HT(c * S + tt * TW, TW), c == 0, c == NCH - 1)
            for c in range(NCH):
                mm(PS[bg](0, TW), WSL[5 + 2 * j](c * 128, 128), HT(c * S + tt * TW, TW), c == 0, c == NCH - 1)
            sig = TF[(j * NTT + tt) % 2](0, TW)
            act(sig, PS[bg](0, TW), AF.Sigmoid)
            tt_op("dve", U(j * UW + 15 + tt * TW, TW), PS[bv](0, TW), sig, ALU.mult)

    load_group(0)
    load_group(1)
    transpose_group(0)
    transpose_group(1)
    load_group(2)
    load_group(3)
    norm_tile_to_ht(0, 0)
    transpose_group(2)
    norm_tile_to_ht(1, 0)
    transpose_group(3)
    for j in range(4):
        memset("pool", U(j * UW, 15), 0.0)
        memset("pool", U(j * UW + 15 + S, 15), 0.0)
    w_in_tile(0)
    norm_tile_to_ht(2, 0)
    w_in_tile(1)
    norm_tile_to_ht(3, 0)
    w_in_tile(2)
    w_in_tile(3)

    WOB = [(0, 384), (384, 384), (768, 256)]
    for k, (c0, ncl) in enumerate(WOB):
        wblock_dma(WBS[k](0, NCH * ncl), w_out_in, c0, ncl, ("wb", k))

    WDS = [T(WDb, i * 2048, BF16) for i in range(6)]
    dft_state = {"next": 0}

    def dft_prefetch(upto):
        while dft_state["next"] < min(upto, 32):
            i = dft_state["next"]
            v = WDS[i % 6](0, 1024)
            dma("sp", v.ap, dft_in[(i // 8) * 16 + (i % 8)], ("wds", i % 6), writes=[v])
            dft_state["next"] += 1

    for kt in range(4):
        v = CX(kt * 512, 512)
        dma("sp", v.ap[0:1, :], dft_in[kt * 16 + 8][0:1, 0:512], ("cx", kt), writes=[v])

    dft_prefetch(6)

    DS = [T(HTb, 0, BF16), T(HTb, 7936, BF16)]
    CT0 = 15872

    def cset(k):
        base = CT0 + k * 5120
        return dict(cb=T(HTb, base, F32), cbb=T(HTb, base + 2048, BF16), rs=T(HTb, base + 3072, F32))

    CS = [cset(k) for k in range(3)]
    tiles = [(j, tt) for j in range(4) for tt in range(NTT)]
    pbank = {}

    def build_D(j):
        for t in range(31):
            dst = DS[j % 2](t * 128, 128)
            sc = SM(C_CW + j * 31 + t, 1)
            ts("dve", dst, IDB(0, 128), sc, None, ALU.mult)

    def conv_s0(i):
        j, tt = tiles[i]
        if i == 0:
            build_D(0)
        if tt == 0 and j + 1 < 4:
            build_D(j + 1)
        b = P.bank()
        for t in range(31):
            mm(PS[b](0, TW), DS[j % 2](t * 128, 128), U(j * UW + tt * TW + t, TW), t == 0, t == 30)
        cs = CS[i % 3]
        ts("dve", cs["cb"](0, TW), PS[b](0, TW), SM(C_CB + j, 1), None, ALU.add)
        copy("dve", cs["cbb"](0, TW), cs["cb"](0, TW))

    def conv_s2(i):
        cs = CS[i % 3]
        b = P.bank()
        mm(PS[b](0, TW), ONES(0, 128), cs["cbb"](0, TW), True, True)
        stt("dve", cs["cb"](0, TW), PS[b](0, TW), -1.0 / 128, cs["cb"](0, TW), ALU.mult, ALU.add)
        tt_op("dve", cs["cbb"](0, TW), cs["cb"](0, TW), cs["cb"](0, TW), ALU.mult)

    def conv_s3(i):
        j, tt = tiles[i]
        cs = CS[i % 3]
        b = P.bank()
        mm(PS[b](0, TW), ONES(0, 128), cs["cbb"](0, TW), True, True)
        act(cs["rs"](0, TW), PS[b](0, TW), AF.Ln, scale=1.0 / 128, bias=EPS(1, 1))
        act(cs["rs"](0, TW), cs["rs"](0, TW), AF.Exp, scale=-0.5)
        tt_op("dve", cs["cb"](0, TW), cs["cb"](0, TW), cs["rs"](0, TW), ALU.mult)
        act(YB(j * S + tt * TW, TW), cs["cb"](0, TW), AF.Silu, scale=SM(C_LG + j, 1), bias=SM(C_LB + j, 1))

    for i in range(16 + 2):
        if i < 16:
            conv_s0(i)
        if 0 <= i - 1 < 16:
            conv_s2(i - 1)
        if 0 <= i - 2 < 16:
            conv_s3(i - 2)

    for h in range(4):
        b = P.bank()
        mm(PS[b](0, 128), SM(C_CD, 128), FM(h * 128, 128), True, True)
        mm(PS[b](128, 128), SM(C_SD, 128), FM(h * 128, 128), True, True)
        copy("dve", CSM(h * 256, 256), PS[b](0, 256))
    Y = T(HTb, 0, BF16)
    AP_ = T(Rb, 16384, BF16)
    AM_ = T(Rb, 24576, BF16)

    def rev(tv, lo, n):
        b0 = tv.off + lo * tv.esz
        b1 = b0 + n * tv.esz
        c0, c1 = b0 // tv.buf.besz, b1 // tv.buf.besz
        assert c0 >= 1
        return View(tv.buf.h[:, c1 - 1:c0 - 1:-1], tv.buf, b0, b1)

    for h in range(4):
        tt_op("dve", AP_(h * 1024 + 1, 511), A(h * S + 1, 511), rev(A, h * S + 1537, 511), ALU.add)
        tt_op("dve", AM_(h * 1024 + 1, 511), A(h * S + 1, 511), rev(A, h * S + 1537, 511), ALU.subtract)
        tt_op("dve", AP_(h * 1024 + 512, 512), A(h * S + 512, 512), rev(A, h * S + 1025, 512), ALU.add)
        tt_op("dve", AM_(h * 1024 + 512, 512), A(h * S + 512, 512), rev(A, h * S + 1025, 512), ALU.subtract)
        copy("act", AP_(h * 1024, 1), A(h * S, 1))
        memset("pool", AM_(h * 1024, 1), 0.0)
    n_ev = 0
    for mt in range(8):
        for hp in range(2):
            b = P.bank()
            for hh in range(2):
                h = 2 * hp + hh
                mm(PS[b](hh * 256, 128), AP_(h * 1024 + mt * 128, 128), CSM(h * 256, 128), True, True)
                mm(PS[b](hh * 256 + 128, 128), AM_(h * 1024 + mt * 128, 128), CSM(h * 256 + 128, 128), True, True)
            copy("act" if n_ev % 2 else "dve", Y(mt * 1024 + hp * 512, 512), PS[b](0, 512))
            n_ev += 1
    b = P.bank()

    def p0(v):
        return View(v.ap[0:1, :], v.buf, v.lo, v.hi)

    for h in range(4):
        mm(p0(PS[b](h * 128, 128)), A(h * S + 1024, 1), CSM(h * 256, 128), True, True)
    copy("dve", p0(Y(8 * 1024, 512)), p0(PS[b](0, 512)))
    YA = A
    for kt in range(4):
        banks = [P.bank() for _ in range(4)]
        for mt in range(8):
            i = kt * 8 + mt
            dft_prefetch(i + 6)
            tab = WDS[i % 6]
            for h in range(4):
                mm(PS[banks[h]](0, TW), Y(mt * 1024 + h * 256, 128), tab(0, 512), mt == 0, False)
                mm(PS[banks[h]](0, TW), Y(mt * 1024 + h * 256 + 128, 128), tab(512, 512), False, False)
        for h in range(4):
            mm(PS[banks[h]](0, TW), p0(Y(8 * 1024 + h * 128, 128)), p0(CX(kt * 512, 512)), False, True)
        for h in range(4):
            copy("act" if h % 2 else "dve", YA(h * S + kt * TW, TW), PS[banks[h]](0, TW))

    ffn_load_wd(0, 0)

    for tt in range(NTT):
        for dc in range(NCH):
            blk = (dc * 128) // 384
            off = (dc * 128) % 384
            ncl = WOB[blk][1]
            b = P.bank()
            for kc in range(NCH):
                rhs = YA(kc * S + tt * TW, TW) if kc < 4 else YB((kc - 4) * S + tt * TW, TW)
                mm(PS[b](0, TW), WBS[blk](kc * ncl + off, 128), rhs, kc == 0, kc == NCH - 1)
            xs = XT(dc * S + tt * TW, TW)
            tt_op("dve", xs, xs, PS[b](0, TW), ALU.add)
        if debug_stop != "mix0":
            norm_tile_to_ht(tt, 1)

    if debug_stop == "mix0":
        return finish_debug(nc, stack, P, XT, out_d)

    ffn_load_gu(0, 0, 0)
    ffn_load_gu(0, 0, 1)

    RSTD = T(Rb, 24704, F32)
    PM = T(WDb, 8192, BF16)

    def load_pm():
        v = PM(0, 2048)
        dma("pool", v.ap.rearrange("p (a n) -> p a n", a=8), pmap_in.rearrange("g (kc p) n -> p (g kc) n", p=128),
            ("pm",), writes=[v])

    def pool_norm_hook(tt):
        rstd_tile(tt, RSTD(tt * TW, TW))

    ffn(0, pre_last_q=load_pm, hook=pool_norm_hook)

    if debug_stop == "ffn0":
        return finish_debug(nc, stack, P, XT, out_d)

    ffn_load_gu(1, 0, 0)
    ffn_load_gu(1, 0, 1)
    HBW = S + 16
    PG = T(Rb, 0, BF16)
    HB = [T(Rb, 8192, BF16), T(Rb, 8192 + 4160, BF16)]
    ET = TSP
    for k in range(2):
        memset("pool", HB[k](0, 8), 0.0)
        memset("pool", HB[k](8 + S, 8), 0.0)
    def compute_h(c):
        Hc = HB[c % 2]
        for tt in range(NTT):
            stt("dve", Hc(8 + tt * TW, TW), XT(c * S + tt * TW, TW), gain(2, c), RSTD(tt * TW, TW), ALU.mult, ALU.mult)

    for c in range(NCH):
        gi = c // 2
        cc = c % 2
        w = 2 << gi
        half = w // 2
        H = HB[c % 2]
        if c == 0:
            compute_h(0)
        pbanks = []
        for tt in range(NTT):
            b = P.bank()
            pbanks.append(b)
            for t in range(w):
                mm(PS[b](0, TW), IDB(0, 128), H(8 + tt * TW + t - half, TW), t == 0, t == w - 1)
        if c + 1 < NCH and (c + 1) % 2 == 1:
            compute_h(c + 1)
        for tt in range(NTT):
            b = pbanks[tt]
            stt("dve", PG(cc * S + tt * TW, TW), PS[b](0, TW), 1.0 / w, H(8 + tt * TW, TW), ALU.mult, ALU.subtract)
            if tt == 0:
                tt_op("dve", ET(0, 8), PS[b](0, 8), SM(C_IC + gi * 16, 8), ALU.mult)
                tt_op("pool", PG(cc * S, 8), ET(0, 8), H(8, 8), ALU.subtract)
            if tt == NTT - 1:
                tt_op("dve", ET(8, 8), PS[b](TW - 8, 8), SM(C_IC + gi * 16 + 8, 8), ALU.mult)
                tt_op("pool", PG(cc * S + S - 8, 8), ET(8, 8), H(8 + S - 8, 8), ALU.subtract)
        if cc == 1:
            first = True
            for tt in range(NTT):
                for dcl in range(2):
                    co = gi * 2 + dcl
                    b = P.bank()
                    for kc in range(2):
                        mm(PS[b](0, TW), PM((gi * 2 + kc) * 256 + dcl * 128, 128), PG(kc * S + tt * TW, TW), kc == 0, kc == 1)
                    xs = XT(co * S + tt * TW, TW)
                    stt("dve", xs, PS[b](0, TW), SM(C_PS + co, 1), xs, ALU.mult, ALU.add)
                    if first and c + 1 < NCH:
                        compute_h(c + 1)
                    first = False
                if gi == 3 and debug_stop != "mix1":
                    norm_tile_to_ht(tt, 3)

    if debug_stop == "mix1":
        return finish_debug(nc, stack, P, XT, out_d)

    ffn_load_wd(1, 0)

    GB = T(Rb, 24576, F32)
    OUTT = [T(Rb, 36864, F32), T(Rb, 40960, F32)]
    ov = out_d.rearrange("(t p) d -> t p d", p=128)
    out_toks = []

    def load_gb():
        v = GB(0, D)
        dma("sp", v.ap, gbc_in, ("gb",), writes=[v])

    def final_hook(tt):
        bss = [P.bank() for _ in range(4)]
        for c in range(NCH):
            sq = SQ[c % 3](0, TW)
            act(sq, XT(c * S + tt * TW, TW), AF.Square)
            for t4 in range(4):
                mm(PS[bss[t4]](0, 1), SQ[c % 3](t4 * 128, 128), ONES(0, 1), c == 0, c == NCH - 1)
        k2 = tt % 2
        l1 = TSP(k2 * 16, 4)
        l2 = TSP(k2 * 16 + 4, 4)
        r4 = TSP(k2 * 16 + 8, 4)
        for t4 in range(4):
            act(TSP(k2 * 16 + t4, 1), PS[bss[t4]](0, 1), AF.Ln, scale=1.0 / D, bias=EPS(0, 1))
        ts("dve", l2, l1, -0.5, None, ALU.mult)
        act(r4, l2, AF.Exp)
        for t4 in range(4):
            ti = tt * 4 + t4
            k = ti % 2
            for hf in range(2):
                b = P.bank()
                for cc in range(4):
                    c = hf * 4 + cc
                    tr(PS[b](cc * 128, 128), XT(c * S + ti * 128, 128))
                stt("dve", OUTT[k](hf * TW, TW), PS[b](0, TW), TSP(k2 * 16 + 8 + t4, 1), GB(hf * TW, TW), ALU.mult, ALU.mult)
            o = OUTT[k](0, D)
            out_toks.append(dma("sp", ov[ti], o.ap, ("out", k), reads=[o]))

    if debug_stop == "ffn1":
        ffn(1)
        return finish_debug(nc, stack, P, XT, out_d)
    ffn(1, pre_last_q=load_gb, hook=final_hook)
    P.add("sp", None, extra_deps=out_toks)
    P.emit(nc, stack)
    stack.close()
    return nc


def finish_debug(nc, stack, P, XT, out_d):
    toks = []
    ov = out_d.rearrange("(c p) (a n) -> c p (a n)", p=128, a=1)
    ov2 = out_d.rearrange("(c h p) n -> c h p n", c=8, h=2)
    for c in range(NCH):
        for h in range(2):
            v = XT(c * S + h * 1024, 1024)
            toks.append(P.add("sp", (lambda e, v=v, c=c, h=h: e.dma_start(out=ov2[c, h], in_=v.ap)), reads=[v],
                              dma_key=("out", (c * 2 + h) % 2)))
    P.add("sp", None, extra_deps=toks)
    P.emit(nc, stack)
    stack.close()
    return nc


def _consts():
    n = np.arange(S, dtype=np.int64)
    ang = 2.0 * np.pi * ((n[:, None] * n[None, :]) % S).astype(np.float64) / S
    Cs = np.cos(ang)
    Sn = -np.sin(ang)
    tab = np.empty((4, 16, 128, 2, 512), dtype=np.float32)
    for kt in range(4):
        for nt in range(16):
            tab[kt, nt, :, 0, :] = Cs[nt * 128:(nt + 1) * 128, kt * 512:(kt + 1) * 512]
            tab[kt, nt, :, 1, :] = Sn[nt * 128:(nt + 1) * 128, kt * 512:(kt + 1) * 512]
    tab = tab.reshape(64, 128, 1024).astype(ml_dtypes.bfloat16)
    e = np.arange(128, dtype=np.int64)
    ang = 2.0 * np.pi * ((e[:, None] * e[None, :]) % 128).astype(np.float64) / 128
    Cd = (np.cos(ang) / 512.0).astype(np.float32)
    Sd = (np.sin(ang) / 512.0).astype(np.float32)
    ic = np.zeros((4, 16), dtype=np.float32)
    for gi in range(4):
        w = 2 << gi
        for i in range(16):
            pos = i if i < 8 else S - 16 + i
            lo = min(max(pos - w // 2, 0), S)
            hi = min(max(pos - w // 2 + w, 0), S)
            ic[gi, i] = 1.0 / float(hi - lo)
    return tab, Cd, Sd, ic


def _prep(inputs):
    tab, Cd, Sd, ic = _consts()
    f = lambda a: np.ascontiguousarray(np.asarray(a, dtype=np.float32))
    sm = np.zeros((128, NS), dtype=np.float32)
    sm[:, C_ID:C_ID + 128] = np.eye(128, dtype=np.float32)
    sm[:, C_CD:C_CD + 128] = Cd
    sm[:, C_SD:C_SD + 128] = Sd
    nm, nf_ = f(inputs["norm_mix_g"]), f(inputs["norm_ffn_g"])
    for si, g in enumerate([nm[0], nf_[0], nm[1], nf_[1]]):
        sm[:, C_GAIN + si * 8:C_GAIN + si * 8 + 8] = g.reshape(8, 128).T
    cw = f(inputs["conv_w"])[0]
    sm[:, C_CW:C_CW + 124] = cw.reshape(31, 4, 128).transpose(2, 1, 0).reshape(128, 124)
    sm[:, C_CB:C_CB + 4] = f(inputs["conv_b"])[0].reshape(4, 128).T
    sm[:, C_LG:C_LG + 4] = f(inputs["conv_ln_g"])[0].reshape(4, 128).T
    sm[:, C_LB:C_LB + 4] = f(inputs["conv_ln_b"])[0].reshape(4, 128).T
    sm[:, C_PS:C_PS + 8] = f(inputs["pool_scale"])[0].reshape(8, 128).T
    sm[:, C_IC:C_IC + 64] = np.broadcast_to(ic.reshape(1, 64), (128, 64))
    gbc = np.ascontiguousarray(np.broadcast_to(f(inputs["final_g"]).reshape(1, D), (128, D)))
    shared = dict(
        smalls=sm, gbc=gbc, dft=tab,
        w_in=f(inputs["w_in_ab"])[0], fmap=f(inputs["fnet_map"])[0], w_out=f(inputs["w_out_ab"])[0],
        pmap=f(inputs["pool_map"])[0], wg=f(inputs["ffn_w_gate"]), wu=f(inputs["ffn_w_up"]), wd=f(inputs["ffn_w_down"]),
    )
    x = f(inputs["x"])
    return [dict(shared, x=x[b]) for b in range(8)]


_NC_CACHE = {}


def kernel(**inputs):
    in_maps = _prep(inputs)
    if "nc" not in _NC_CACHE:
        _NC_CACHE["nc"] = build_program()
    nc = _NC_CACHE["nc"]
    res = run_bass_kernel_spmd(nc, in_maps, core_ids=list(range(8)))
    return np.stack([np.asarray(r["out"], dtype=np.float32) for r in res.results], axis=0)
```
